# Optimizing a Trainium2 kernel written in Bass

```python
import math, functools
import jax, jax.numpy as jnp
from jax import lax
import numpy as np

D_MODEL = 2048
BATCH = 16
SEQ = 2048
DEPTH = 1
DEC_BATCH = 128
DEC_SEQ = 1
PAST_LEN = 16384
PAGE_SIZE = 128

GDN_HEADS = 8
GDN_DK = 128
GDN_DV = 128
GDN_CONV = 4
GDN_CHUNK = 64
SWA_HEADS = 16
SWA_KV_HEADS = 4
SWA_HD = 64
WINDOW = 128
ROPE_THETA = 10000.0
MEM_TOKENS = 256
MEM_HEADS = 4
MEM_HD = 128
D_FF = 5632
EPS = 1e-6
L2_EPS = 1e-6

GDN_QK_W = GDN_HEADS * GDN_DK
GDN_V_W = GDN_HEADS * GDN_DV
CONV_CH = 2 * GDN_QK_W + GDN_V_W
SWA_Q_W = SWA_HEADS * SWA_HD
SWA_KV_W = SWA_KV_HEADS * SWA_HD
SWA_GROUP = SWA_HEADS // SWA_KV_HEADS
MEM_W = MEM_HEADS * MEM_HD
IN_SIZES = (CONV_CH, GDN_V_W, GDN_HEADS, GDN_HEADS, SWA_Q_W, SWA_KV_W, SWA_KV_W, D_MODEL, D_MODEL)
D_IN = CONV_CH + GDN_V_W + 2 * GDN_HEADS + SWA_Q_W + 2 * SWA_KV_W + 2 * D_MODEL

kernel_name = 'hybrid_gdn_swa_sink_macaron_decode_step'


def rms_norm(x, g):
    xf = x.astype(jnp.float32)
    y = xf * lax.rsqrt(jnp.mean(xf * xf, -1, keepdims=True) + EPS)
    return (y * g.astype(jnp.float32)).astype(x.dtype)


def l2_normalize(x):
    xf = x.astype(jnp.float32)
    return xf * lax.rsqrt(jnp.sum(xf * xf, -1, keepdims=True) + L2_EPS)


def swiglu(h, w1, w3, w2):
    return (jax.nn.silu(h @ w1) * (h @ w3)) @ w2


def rope(x, pos):
    half = x.shape[-1] // 2
    inv_freq = ROPE_THETA ** (-jnp.arange(half, dtype=jnp.float32) / half)
    ang = pos.astype(jnp.float32)[:, None] * inv_freq[None, :]
    cos = jnp.cos(ang)[:, None, :]
    sin = jnp.sin(ang)[:, None, :]
    xf = x.astype(jnp.float32)
    x1, x2 = xf[..., :half], xf[..., half:]
    return jnp.concatenate([x1 * cos - x2 * sin, x2 * cos + x1 * sin], -1).astype(x.dtype)


def causal_conv(xc, buf, w):
    T = xc.shape[1]
    full = jnp.concatenate([buf.astype(xc.dtype), xc], 1)
    y = full[:, 0:T] * w[0]
    for i in range(1, GDN_CONV):
        y = y + full[:, i:i + T] * w[i]
    return jax.nn.silu(y), full[:, -(GDN_CONV - 1):]


def gated_delta_rule(q, k, v, g, beta, s0):
    B, T, H, DK = q.shape
    DV = v.shape[-1]
    C = math.gcd(T, GDN_CHUNK)
    N = T // C
    f32 = jnp.float32

    def chunkify(a):
        return a.astype(f32).reshape(B, N, C, H, a.shape[-1]).transpose(1, 0, 3, 2, 4)

    qc, kc, vc = chunkify(q), chunkify(k), chunkify(v)
    gc = g.astype(f32).reshape(B, N, C, H).transpose(1, 0, 3, 2)
    bc = beta.astype(f32).reshape(B, N, C, H).transpose(1, 0, 3, 2)
    G = jnp.cumsum(gc, -1)
    diff = G[..., :, None] - G[..., None, :]
    idx = jnp.arange(C)
    strict = idx[:, None] > idx[None, :]
    incl = idx[:, None] >= idx[None, :]
    dec_strict = jnp.where(strict, jnp.exp(jnp.where(strict, diff, 0.0)), 0.0)
    dec_incl = jnp.where(incl, jnp.exp(jnp.where(incl, diff, 0.0)), 0.0)
    kk = jnp.einsum('nbhid,nbhjd->nbhij', kc, kc)
    lmat = jnp.eye(C, dtype=f32) + bc[..., :, None] * kk * dec_strict
    gamma = jnp.exp(G)[..., None]
    rhs = jnp.concatenate([bc[..., None] * vc, bc[..., None] * gamma * kc], -1)
    sol = lax.linalg.triangular_solve(lmat, rhs, left_side=True, lower=True, unit_diagonal=True)
    u_v, w_k = sol[..., :DV], sol[..., DV:]
    qk = jnp.einsum('nbhid,nbhjd->nbhij', qc, kc) * dec_incl
    q_g = qc * gamma
    k_end = kc * jnp.exp(G[..., -1:] - G)[..., None]
    g_end = jnp.exp(G[..., -1])

    def step(S, xs):
        u_v_n, w_n, qk_n, q_g_n, k_end_n, g_end_n = xs
        u = u_v_n - jnp.einsum('bhcd,bhde->bhce', w_n, S)
        o = jnp.einsum('bhcd,bhde->bhce', q_g_n, S) + jnp.einsum('bhij,bhje->bhie', qk_n, u)
        S = g_end_n[..., None, None] * S + jnp.einsum('bhcd,bhce->bhde', k_end_n, u)
        return S, o

    S, o = lax.scan(step, s0.astype(f32), (u_v, w_k, qk, q_g, k_end, g_end))
    o = o.transpose(1, 0, 3, 2, 4).reshape(B, T, H, DV)
    return o, S


def sink_softmax(s, sink):
    m = jnp.maximum(jnp.max(s, -1, keepdims=True), sink)
    p = jnp.exp(s - m)
    denom = jnp.sum(p, -1, keepdims=True) + jnp.exp(sink - m)
    return p / denom


def swa_prompt(q, k, v, sinks):
    B, T, _, D = q.shape
    NB = T // WINDOW
    qb = q.reshape(B, NB, WINDOW, SWA_KV_HEADS, SWA_GROUP, D)
    kb = k.reshape(B, NB, WINDOW, SWA_KV_HEADS, D)
    vb = v.reshape(B, NB, WINDOW, SWA_KV_HEADS, D)
    kk = jnp.concatenate([jnp.concatenate([jnp.zeros_like(kb[:, :1]), kb[:, :-1]], 1), kb], 2)
    vv = jnp.concatenate([jnp.concatenate([jnp.zeros_like(vb[:, :1]), vb[:, :-1]], 1), vb], 2)
    s = jnp.einsum('bnqhgd,bnkhd->bnhgqk', qb, kk).astype(jnp.float32) * (D ** -0.5)
    qi = jnp.arange(WINDOW)[:, None] + WINDOW
    kj = jnp.arange(2 * WINDOW)[None, :]
    rel = qi - kj
    band = (rel >= 0) & (rel < WINDOW)
    has_prev = jnp.arange(NB)[:, None, None] > 0
    mask = band[None] & ((kj >= WINDOW)[None] | has_prev)
    s = jnp.where(mask[None, :, None, None], s, -jnp.inf)
    p = sink_softmax(s, sinks.astype(jnp.float32).reshape(SWA_KV_HEADS, SWA_GROUP, 1, 1))
    o = jnp.einsum('bnhgqk,bnkhd->bnqhgd', p.astype(vv.dtype), vv).reshape(B, T, SWA_Q_W)
    wb = min(WINDOW, T)
    return o, k[:, -wb:], v[:, -wb:]


def swa_sample(q, k, v, sinks, k_buf, v_buf):
    B, T, _, D = q.shape
    wb = k_buf.shape[1]
    kk = jnp.concatenate([k_buf.astype(k.dtype), k], 1)
    vv = jnp.concatenate([v_buf.astype(v.dtype), v], 1)
    qg = q.reshape(B, T, SWA_KV_HEADS, SWA_GROUP, D)
    s = jnp.einsum('bthgd,bkhd->bhgtk', qg, kk).astype(jnp.float32) * (D ** -0.5)
    rel = (jnp.arange(T)[:, None] + wb) - jnp.arange(wb + T)[None, :]
    mask = (rel >= 0) & (rel < WINDOW)
    s = jnp.where(mask, s, -jnp.inf)
    p = sink_softmax(s, sinks.astype(jnp.float32).reshape(SWA_KV_HEADS, SWA_GROUP, 1, 1))
    o = jnp.einsum('bhgtk,bkhd->bthgd', p.astype(vv.dtype), vv).reshape(B, T, SWA_Q_W)
    return o, kk[:, -wb:], vv[:, -wb:]


def mem_attend(h, mem_k, mem_v, w_q, w_o):
    B, T, _ = h.shape
    q = (h @ w_q).reshape(B, T, MEM_HEADS, MEM_HD)
    s = jnp.einsum('bthd,bmhd->bhtm', q, mem_k.astype(q.dtype)).astype(jnp.float32) * (MEM_HD ** -0.5)
    p = jax.nn.softmax(s, -1).astype(h.dtype)
    o = jnp.einsum('bhtm,bmhd->bthd', p, mem_v.astype(h.dtype)).reshape(B, T, MEM_W)
    return o @ w_o


def mixer_block(h, pos, conv_buf, s0, swa_fn, lp):
    B, T, _ = h.shape
    offsets = np.cumsum(IN_SIZES)[:-1].tolist()
    proj = h @ lp['w_in']
    qkv_c, z, b_raw, a_raw, q_s, k_s, v_s, gate_gdn, gate_swa = jnp.split(proj, offsets, -1)
    qkv_c, conv_new = causal_conv(qkv_c, conv_buf, lp['conv_w'])
    q_g, k_g, v_g = jnp.split(qkv_c, [GDN_QK_W, 2 * GDN_QK_W], -1)
    q_g = l2_normalize(q_g.reshape(B, T, GDN_HEADS, GDN_DK)) * (GDN_DK ** -0.5)
    k_g = l2_normalize(k_g.reshape(B, T, GDN_HEADS, GDN_DK))
    v_g = v_g.reshape(B, T, GDN_HEADS, GDN_DV)
    beta = jax.nn.sigmoid(b_raw.astype(jnp.float32))
    g = -jnp.exp(lp['gdn_A_log'].astype(jnp.float32)) * jax.nn.softplus(a_raw.astype(jnp.float32) + lp['gdn_dt_bias'].astype(jnp.float32))
    o_g, s_new = gated_delta_rule(q_g, k_g, v_g, g, beta, s0)
    o_g = rms_norm(o_g, lp['gdn_norm']) * jax.nn.silu(z.reshape(B, T, GDN_HEADS, GDN_DV).astype(jnp.float32))
    o_g = o_g.reshape(B, T, GDN_V_W).astype(h.dtype)
    q_s = rope(q_s.reshape(B, T, SWA_HEADS, SWA_HD), pos)
    k_s = rope(k_s.reshape(B, T, SWA_KV_HEADS, SWA_HD), pos)
    v_s = v_s.reshape(B, T, SWA_KV_HEADS, SWA_HD)
    o_s, k_buf_new, v_buf_new = swa_fn(q_s, k_s, v_s, lp['swa_sinks'])
    merged = jax.nn.sigmoid(gate_gdn) * (o_g @ lp['w_br_gdn']) + jax.nn.sigmoid(gate_swa) * (o_s @ lp['w_br_swa'])
    return merged @ lp['w_out'], s_new.astype(h.dtype), conv_new, k_buf_new, v_buf_new


def layer_forward(x, pos, conv_buf, s0, swa_fn, mem_k, mem_v, lp):
    x = x + 0.5 * swiglu(rms_norm(x, lp['norm_ffn1']), lp['ffn1_w1'], lp['ffn1_w3'], lp['ffn1_w2'])
    mix, s_new, conv_new, k_buf_new, v_buf_new = mixer_block(rms_norm(x, lp['norm_mix']), pos, conv_buf, s0, swa_fn, lp)
    x = x + mix
    x = x + mem_attend(rms_norm(x, lp['norm_mem_q']), mem_k, mem_v, lp['w_mem_q'], lp['w_mem_o'])
    x = x + 0.5 * swiglu(rms_norm(x, lp['norm_ffn2']), lp['ffn2_w1'], lp['ffn2_w3'], lp['ffn2_w2'])
    return x, s_new, conv_new, k_buf_new, v_buf_new


def setup_inputs(seed: int = 0) -> dict:
    key = jax.random.key(seed)
    keys = iter(jax.random.split(key, 48))
    f32 = jnp.float32
    L = DEPTH
    wb = min(WINDOW, PAST_LEN)

    def normal(shape, scale=1.0):
        return jax.random.normal(next(keys), shape, f32) * scale

    def gain(shape):
        return 1.0 + 0.05 * jax.random.normal(next(keys), shape, f32)

    dt = jnp.exp(jax.random.uniform(next(keys), (L, GDN_HEADS), f32, math.log(1e-3), math.log(1e-1)))
    inp = {}
    inp['x_prompt'] = normal((BATCH, SEQ, D_MODEL))
    inp['x_sample'] = normal((DEC_BATCH, DEC_SEQ, D_MODEL))
    inp['state_gdn'] = normal((L, DEC_BATCH, GDN_HEADS, GDN_DK, GDN_DV), 0.5)
    inp['state_conv'] = normal((L, DEC_BATCH, GDN_CONV - 1, CONV_CH))
    inp['cache_swa_k'] = normal((L, DEC_BATCH, wb, SWA_KV_HEADS, SWA_HD))
    inp['cache_swa_v'] = normal((L, DEC_BATCH, wb, SWA_KV_HEADS, SWA_HD))
    inp['cache_mem_k'] = normal((L, DEC_BATCH, MEM_TOKENS, MEM_HEADS, MEM_HD))
    inp['cache_mem_v'] = normal((L, DEC_BATCH, MEM_TOKENS, MEM_HEADS, MEM_HD))
    inp['mem_prompt'] = normal((BATCH, MEM_TOKENS, D_MODEL))
    inp['norm_ffn1'] = gain((L, D_MODEL))
    inp['ffn1_w1'] = normal((L, D_MODEL, D_FF), D_MODEL ** -0.5)
    inp['ffn1_w3'] = normal((L, D_MODEL, D_FF), D_MODEL ** -0.5)
    inp['ffn1_w2'] = normal((L, D_FF, D_MODEL), D_FF ** -0.5)
    inp['norm_mix'] = gain((L, D_MODEL))
    inp['w_in'] = normal((L, D_MODEL, D_IN), D_MODEL ** -0.5)
    inp['conv_w'] = normal((L, GDN_CONV, CONV_CH), GDN_CONV ** -0.5)
    inp['gdn_A_log'] = jnp.log(jax.random.uniform(next(keys), (L, GDN_HEADS), f32, 1.0, 16.0))
    inp['gdn_dt_bias'] = dt + jnp.log(-jnp.expm1(-dt))
    inp['gdn_norm'] = gain((L, GDN_DV))
    inp['swa_sinks'] = normal((L, SWA_HEADS), 0.5)
    inp['w_br_gdn'] = normal((L, GDN_V_W, D_MODEL), GDN_V_W ** -0.5)
    inp['w_br_swa'] = normal((L, SWA_Q_W, D_MODEL), SWA_Q_W ** -0.5)
    inp['w_out'] = normal((L, D_MODEL, D_MODEL), D_MODEL ** -0.5)
    inp['norm_mem_q'] = gain((L, D_MODEL))
    inp['norm_mem_kv'] = gain((L, D_MODEL))
    inp['w_mem_q'] = normal((L, D_MODEL, MEM_W), D_MODEL ** -0.5)
    inp['w_mem_k'] = normal((L, D_MODEL, MEM_W), D_MODEL ** -0.5)
    inp['w_mem_v'] = normal((L, D_MODEL, MEM_W), D_MODEL ** -0.5)
    inp['w_mem_o'] = normal((L, MEM_W, D_MODEL), MEM_W ** -0.5)
    inp['norm_ffn2'] = gain((L, D_MODEL))
    inp['ffn2_w1'] = normal((L, D_MODEL, D_FF), D_MODEL ** -0.5)
    inp['ffn2_w3'] = normal((L, D_MODEL, D_FF), D_MODEL ** -0.5)
    inp['ffn2_w2'] = normal((L, D_FF, D_MODEL), D_FF ** -0.5)
    inp['norm_final'] = gain((D_MODEL,))
    return inp


def reference(x_prompt, x_sample, state_gdn, state_conv, cache_swa_k, cache_swa_v, cache_mem_k, cache_mem_v,
              mem_prompt, norm_ffn1, ffn1_w1, ffn1_w3, ffn1_w2, norm_mix, w_in, conv_w, gdn_A_log, gdn_dt_bias,
              gdn_norm, swa_sinks, w_br_gdn, w_br_swa, w_out, norm_mem_q, norm_mem_kv, w_mem_q, w_mem_k, w_mem_v,
              w_mem_o, norm_ffn2, ffn2_w1, ffn2_w3, ffn2_w2, norm_final):
    Bp, Tp, _ = x_prompt.shape
    Bs, Ts, _ = x_sample.shape
    n_mem = mem_prompt.shape[1]
    pos_p = jnp.arange(Tp, dtype=jnp.int32)
    pos_s = PAST_LEN + jnp.arange(Ts, dtype=jnp.int32)
    hp, hs = x_prompt, x_sample
    p_gdn, p_conv, p_k, p_v, p_mk, p_mv = [], [], [], [], [], []
    s_gdn, s_conv, s_k, s_v = [], [], [], []
    for l in range(DEPTH):
        lp = {
            'norm_ffn1': norm_ffn1[l], 'ffn1_w1': ffn1_w1[l], 'ffn1_w3': ffn1_w3[l], 'ffn1_w2': ffn1_w2[l],
            'norm_mix': norm_mix[l], 'w_in': w_in[l], 'conv_w': conv_w[l], 'gdn_A_log': gdn_A_log[l],
            'gdn_dt_bias': gdn_dt_bias[l], 'gdn_norm': gdn_norm[l], 'swa_sinks': swa_sinks[l],
            'w_br_gdn': w_br_gdn[l], 'w_br_swa': w_br_swa[l], 'w_out': w_out[l],
            'norm_mem_q': norm_mem_q[l], 'w_mem_q': w_mem_q[l], 'w_mem_o': w_mem_o[l],
            'norm_ffn2': norm_ffn2[l], 'ffn2_w1': ffn2_w1[l], 'ffn2_w3': ffn2_w3[l], 'ffn2_w2': ffn2_w2[l],
        }
        mem_h = rms_norm(mem_prompt, norm_mem_kv[l])
        mk_p = (mem_h @ w_mem_k[l]).reshape(Bp, n_mem, MEM_HEADS, MEM_HD)
        mv_p = (mem_h @ w_mem_v[l]).reshape(Bp, n_mem, MEM_HEADS, MEM_HD)
        conv0 = jnp.zeros((Bp, GDN_CONV - 1, CONV_CH), x_prompt.dtype)
        s0 = jnp.zeros((Bp, GDN_HEADS, GDN_DK, GDN_DV), jnp.float32)
        hp, sg, sc, kb, vb = layer_forward(hp, pos_p, conv0, s0, swa_prompt, mk_p, mv_p, lp)
        p_gdn.append(sg); p_conv.append(sc); p_k.append(kb); p_v.append(vb); p_mk.append(mk_p); p_mv.append(mv_p)
        swa_s = functools.partial(swa_sample, k_buf=cache_swa_k[l], v_buf=cache_swa_v[l])
        hs, sg, sc, kb, vb = layer_forward(hs, pos_s, state_conv[l], state_gdn[l], swa_s, cache_mem_k[l], cache_mem_v[l], lp)
        s_gdn.append(sg); s_conv.append(sc); s_k.append(kb); s_v.append(vb)
    y_prompt = rms_norm(hp, norm_final)
    y_sample = rms_norm(hs, norm_final)
    return (y_prompt, y_sample,
            jnp.stack(p_gdn), jnp.stack(p_conv), jnp.stack(p_k), jnp.stack(p_v), jnp.stack(p_mk), jnp.stack(p_mv),
            jnp.stack(s_gdn), jnp.stack(s_conv), jnp.stack(s_k), jnp.stack(s_v))
```

```python
import concourse.bass as bass
import concourse.mybir as mybir

F32 = mybir.dt.float32
BF16 = mybir.dt.bfloat16
I32 = mybir.dt.int32
ESZ = {F32: 4, BF16: 2, I32: 4, mybir.dt.float32r: 4}
ENGS = ("pe", "act", "dve", "pool", "sp")
SB_PAGE = 512
PS_PAGE = 2048


class _Op:
    __slots__ = ("eng", "fn", "deps", "flag", "rank", "dma", "label", "seq", "est", "pre", "succ", "prio", "npre", "t1")

    def __init__(self, eng, fn, deps, dma=None, label=None):
        self.seq = 0
        self.est = 0.3
        self.eng = eng
        self.fn = fn
        self.deps = deps
        self.flag = False
        self.rank = None
        self.dma = dma
        self.label = label


class Em:
    def __init__(self, nc):
        self.nc = nc
        self.ops = {e: [] for e in ENGS}
        self.pw = {}
        self.pr = {}
        self.dma_tot = {}
        self.tok_w = {}
        self.tok_r = {}
        self.final = []

    @staticmethod
    def pages(a):
        t = a.tensor
        sp = str(a.space) if hasattr(a, "space") else ""
        esz = ESZ[a.dtype]
        row = a.ap[0][0]
        off = a.offset % row if row > 0 else a.offset
        ext = 1
        for st, cnt in a.ap[1:]:
            ext += (cnt - 1) * abs(st)
        lo = off * esz
        hi = (off + ext) * esz
        pg = PS_PAGE if "PSUM" in sp.upper() or "psum" in t.name else SB_PAGE
        return [(t.name, p) for p in range(lo // pg, (hi - 1) // pg + 1)]

    def _collect(self, reads, writes, tr, tw):
        deps = []
        for a in reads:
            for p in self.pages(a):
                d = self.pw.get(p)
                if d is not None:
                    deps.append(d)
                if "psum" in p[0]:
                    deps.extend(self.pr.get(p, ()))
        for a in writes:
            for p in self.pages(a):
                d = self.pw.get(p)
                if d is not None:
                    deps.append(d)
                deps.extend(self.pr.get(p, ()))
        for t in tr:
            d = self.tok_w.get(t)
            if d is not None:
                deps.append(d)
        for t in tw:
            d = self.tok_w.get(t)
            if d is not None:
                deps.append(d)
            deps.extend(self.tok_r.get(t, ()))
        return deps

    def _commit(self, dep, reads, writes, tr, tw):
        for a in reads:
            for p in self.pages(a):
                self.pr.setdefault(p, []).append(dep)
        for a in writes:
            for p in self.pages(a):
                self.pw[p] = dep
                self.pr[p] = []
        for t in tr:
            self.tok_r.setdefault(t, []).append(dep)
        for t in tw:
            self.tok_w[t] = dep
            self.tok_r[t] = []

    def restamp(self, key, aps):
        dep = ("dma", key, self.dma_tot[key])
        for a in aps:
            for p in self.pages(a):
                self.pw[p] = dep

    def op(self, eng, fn, r=(), w=(), tr=(), tw=(), label=None, est=None):
        deps = self._collect(r, w, tr, tw)
        o = _Op(eng, fn, deps, label=label)
        self._seq = getattr(self, "_seq", 0) + 1
        o.seq = self._seq
        if est is None:
            n = 1
            if w:
                n = 1
                for st_, cnt in w[0].ap[1:]:
                    n *= cnt
            est = 0.12 + n / 960.0
        o.est = est
        self.ops[eng].append(o)
        self._commit(("op", o), r, w, tr, tw)
        return o

    def dma(self, q, key, out, in_, r=(), w=(), tr=(), tw=(), final=False, **kw):
        deps = self._collect(r, w, tr, tw)
        tot = self.dma_tot.get(key, 0) + 16
        self.dma_tot[key] = tot
        o = _Op(q, lambda e: e.dma_start(out=out, in_=in_, **kw), deps, dma=(key, tot))
        self._seq = getattr(self, "_seq", 0) + 1
        o.seq = self._seq
        o.est = 6.0
        self.ops[q].append(o)
        dep = ("dma", key, tot)
        self._commit(dep, r, w, tr, tw)
        if final:
            self.final.append(dep)
        return o

    def schedule(self, window=48):
        import heapq
        allops = [o for e in ENGS for o in self.ops[e]]
        dmap = {}
        for o in allops:
            if o.dma is not None:
                dmap[(o.dma[0], o.dma[1])] = o
        for o in allops:
            o.succ = []
        for o in allops:
            pre = {}
            for d in o.deps:
                p = d[1] if d[0] == "op" else dmap.get((d[1], d[2]))
                if p is not None and p is not o:
                    pre[id(p)] = p
            o.pre = list(pre.values())
            o.npre = len(o.pre)
            for p in o.pre:
                p.succ.append(o)
        for o in sorted(allops, key=lambda x: -x.seq):
            o.prio = o.est + max([s.prio for s in o.succ], default=0.0)
        fixed = ("pool", "sp")
        queues = {e: sorted(self.ops[e], key=lambda x: x.seq) for e in ENGS}
        ready = {e: [] for e in ENGS}
        ptr = {e: 0 for e in fixed}
        for e in ENGS:
            if e in fixed:
                continue
            for o in queues[e]:
                if o.npre == 0:
                    heapq.heappush(ready[e], (o.seq, id(o), o))
        free = {e: 0.0 for e in ENGS}
        order = {e: [] for e in ENGS}
        done = 0
        total = len(allops)
        INF = float("inf")
        while done < total:
            best = None
            for e in ENGS:
                if e in fixed:
                    if ptr[e] >= len(queues[e]):
                        continue
                    o = queues[e][ptr[e]]
                    if o.npre > 0:
                        continue
                    st = max([free[e]] + [p.t1 for p in o.pre])
                    cand = (st, -o.prio, o.seq, e, o)
                else:
                    if not ready[e]:
                        continue
                    cs = heapq.nsmallest(window, ready[e])
                    cand = None
                    for (_, _, o) in cs:
                        st = max([free[e]] + [p.t1 for p in o.pre])
                        k = (st, -o.prio, o.seq, e, o)
                        if cand is None or k[:3] < cand[:3]:
                            cand = k
                if best is None or cand[:3] < best[:3]:
                    best = cand
            assert best is not None, "scheduler deadlock"
            st, _, _, e, o = best
            if e in fixed:
                ptr[e] += 1
                issue = 1.3 if e == "pool" else 0.1
                free[e] = st + (issue if o.dma is not None else 0.0)
                o.t1 = st + (o.est if o.dma is not None else 0.0)
            else:
                ready[e].remove((o.seq, id(o), o))
                heapq.heapify(ready[e])
                free[e] = st + o.est
                o.t1 = st + o.est + 0.25
            order[e].append(o)
            done += 1
            for s in o.succ:
                s.npre -= 1
                if s.npre == 0 and s.eng not in fixed:
                    heapq.heappush(ready[s.eng], (s.seq, id(s), s))
        for e in ENGS:
            self.ops[e] = order[e]
        self.sim_time = max(free.values())

    def build(self, final_eng="sp"):
        nc = self.nc
        fin = _Op(final_eng, None, list(self.final))
        self._seq = getattr(self, "_seq", 0) + 1
        fin.seq = self._seq
        fin.est = 0.0
        self.ops[final_eng].append(fin)
        import os as _os2
        if int(_os2.environ.get("DBG_SCHED", "1")):
            self.schedule(int(_os2.environ.get("DBG_WINDOW", "48")))
        for e in ENGS:
            for o in self.ops[e]:
                for d in o.deps:
                    if d[0] == "op":
                        if d[1].eng == "pe" and o.eng == "pe":
                            continue
                        d[1].flag = True
        for e in ENGS:
            n = 0
            for o in self.ops[e]:
                if o.dma is None and o.flag:
                    n += 1
                    o.rank = n
        keys = sorted(self.dma_tot.keys())
        import contextlib
        with contextlib.ExitStack() as st:
            esem = {e: st.enter_context(nc.semaphore("s_" + e)) for e in ENGS}
            dsem = {k: st.enter_context(nc.semaphore("d_" + k)) for k in keys}
            block = st.enter_context(nc.Block())
            ops = self.ops

            def replay(ename, eng):
                waited = {}
                for o in ops[ename]:
                    need = {}
                    for d in o.deps:
                        if d[0] == "op":
                            po = d[1]
                            if po.eng == "pe" and ename == "pe":
                                continue
                            s = esem[po.eng]
                            v = po.rank
                            k = "e_" + po.eng
                        else:
                            s = dsem[d[1]]
                            v = d[2]
                            k = "d_" + d[1]
                        if need.get(k, (None, 0))[1] < v:
                            need[k] = (s, v)
                    for k, (s, v) in need.items():
                        if waited.get(k, 0) >= v:
                            continue
                        waited[k] = v
                        eng.wait_ge(s, v)
                    if o.fn is None:
                        continue
                    ins = o.fn(eng)
                    if o.dma is not None:
                        ins.then_inc(dsem[o.dma[0]], 16)
                    elif o.flag:
                        ins.then_inc(esem[ename], 1)

            @block.tensor
            def _(eng):
                replay("pe", eng)

            @block.scalar
            def _(eng):
                replay("act", eng)

            @block.vector
            def _(eng):
                replay("dve", eng)

            @block.gpsimd
            def _(eng):
                replay("pool", eng)

            @block.sync
            def _(eng):
                replay("sp", eng)

import numpy as np
import contextlib
from concourse.bass_utils import run_bass_kernel_spmd

AF = mybir.ActivationFunctionType
ALU = mybir.AluOpType

D = 2048
KC = 16
DFF = 5632
EPS = 1e-6
SW_Q0, SW_K0, SW_V0 = 4112, 5136, 5392
GG0, GS0 = 5648, 7696
NORM_FFN1, NORM_MIX, NORM_MEMQ, NORM_MEMKV, NORM_FFN2, NORM_FINAL = range(6)


import os as _os0
FP32R = int(_os0.environ.get("DBG_FP32R", "0"))


def _is_ap(x):
    return hasattr(x, "tensor") and hasattr(x, "ap")


class Ctx:
    def __init__(self, nc):
        self.nc = nc
        self.em = Em(nc)
        self._bank = 0
        self._bset = 0
        self._slot = 0
        self.reserved = set()
        self.scratch = None
        self.scr_index = {}
        self.scratch_n = 0

    def mm(self, out, lhsT, rhs, start=True, stop=True, fr=False):
        l2, r2 = lhsT, rhs
        if fr and FP32R:
            l2 = lhsT.bitcast(mybir.dt.float32r)
            r2 = rhs.bitcast(mybir.dt.float32r)
        n = 1
        for st_, cnt in rhs.ap[1:]:
            n *= cnt
        est = 0.03 + n / 1850.0 * (4.0 if rhs.dtype == F32 and not (fr and FP32R) else 1.0)
        self.em.op("pe", lambda e: e.matmul(out, l2, r2, start=start, stop=stop), r=[lhsT, rhs], w=[out], est=est)

    def tr(self, out, in_, ident):
        self.em.op("pe", lambda e: e.transpose(out, in_, ident), r=[in_, ident], w=[out], est=0.1)

    def act(self, out, in_, func, bias=None, scale=None, accum=None):
        kw = {}
        r = [in_]
        w = [out]
        if bias is not None:
            kw["bias"] = bias
            if _is_ap(bias):
                r.append(bias)
        if scale is not None:
            kw["scale"] = scale
            if _is_ap(scale):
                r.append(scale)
        if accum is not None:
            kw["accum_out"] = accum
            w.append(accum)
        self.em.op("act", lambda e: e.activation(out, in_, func, **kw), r=r, w=w)

    def tt(self, eng, out, a, b, op):
        self.em.op(eng, lambda e: e.tensor_tensor(out=out, in0=a, in1=b, op=op), r=[a, b], w=[out])

    def ts(self, eng, out, a, s1, op0, s2=None, op1=None):
        r = [a] + [s for s in (s1, s2) if _is_ap(s)]
        if op1 is None:
            self.em.op(eng, lambda e: e.tensor_scalar(out, a, s1, None, op0), r=r, w=[out])
        else:
            self.em.op(eng, lambda e: e.tensor_scalar(out, a, s1, s2, op0, op1), r=r, w=[out])

    def stt(self, out, a, scalar, b, op0, op1):
        r = [a, b] + ([scalar] if _is_ap(scalar) else [])
        self.em.op("dve", lambda e: e.scalar_tensor_tensor(out=out, in0=a, scalar=scalar, in1=b, op0=op0, op1=op1), r=r, w=[out])

    def cp(self, eng, out, in_):
        if eng == "act":
            self.em.op("act", lambda e: e.copy(out, in_), r=[in_], w=[out])
        else:
            self.em.op(eng, lambda e: e.tensor_copy(out=out, in_=in_), r=[in_], w=[out])

    def rsum(self, out, in_):
        self.em.op("dve", lambda e: e.reduce_sum(out=out, in_=in_, axis=mybir.AxisListType.X), r=[in_], w=[out])

    def memset(self, eng, ap, v):
        self.em.op(eng, lambda e: e.memset(ap, v), w=[ap])

    def bank(self):
        while True:
            b = self._bank
            self._bank = (b + 1) % 8
            if b not in self.reserved:
                return self.ps[:, b, :]

    def bankset(self):
        b = self._bset
        self._bset = 4 - b
        return b

    def slot(self):
        i = self._slot
        self._slot = (i + 1) % len(self.wslots)
        return self.wslots[i], "ws%d" % i

    def load_w(self, W, k0, kn, pieces):
        merged = []
        for p in pieces:
            if merged and merged[-1][0] + merged[-1][2] == p[0] and merged[-1][1] + merged[-1][2] == p[1]:
                merged[-1][2] += p[2]
            else:
                merged.append(list(p))
        out = []
        for b0 in range(0, kn, 8):
            bn = min(8, kn - b0)
            slot, key = self.slot()
            bkey = (W.tensor.name, k0 + b0, bn, tuple(tuple(m) for m in merged))
            use_scr = self.scratch is not None and bkey in self.scr_index
            if use_scr:
                idx = self.scr_index[bkey]
                self.em.dma("pool", key, slot[:, :, :], self.scratch[idx], w=[slot[:, :, :]], tr=["scr%d" % idx])
            else:
                aps = []
                for (o2, c0, w) in merged:
                    dst = slot[:, 0:bn, o2:o2 + w]
                    src = W[(k0 + b0) * 128:(k0 + b0 + bn) * 128, c0:c0 + w].rearrange("(kc p) f -> p kc f", p=128)
                    self.em.dma("pool", key, dst, src, w=[dst])
                    aps.append(dst)
                self.em.restamp(key, aps)
                if self.scratch is not None and len(self.scr_index) < self.scratch_n:
                    idx = len(self.scr_index)
                    self.scr_index[bkey] = idx
                    self.em.dma("sp", "scrst%d" % (idx % 6), self.scratch[idx], slot[:, :, :], r=[slot[:, :, :]], tw=["scr%d" % idx])
            for j in range(bn):
                out.append(slot[:, j, :])
        return out

    def dense(self, W, nk, subs, rhs, N, evac):
        groups = [subs[i:i + 4] for i in range(0, len(subs), 4)]
        for gi, grp in enumerate(groups):
            base = self.bankset()
            self.reserved = set(range(base, base + len(grp)))
            pieces = []
            for si, sub in enumerate(grp):
                o2 = si * 128
                for (c0, w) in sub:
                    pieces.append((o2, c0, w))
                    o2 += w
            for k0 in range(0, nk, 8):
                kn = min(8, nk - k0)
                wt = self.load_w(W, k0, kn, pieces)
                for si, sub in enumerate(grp):
                    wd = sum(w for _, w in sub)
                    for kk in range(kn):
                        self.mm(self.ps[0:wd, base + si, 0:N], wt[kk][:, si * 128:si * 128 + wd], rhs(k0 + kk),
                                start=(k0 + kk == 0), stop=(k0 + kk == nk - 1))
                    if k0 + kn == nk:
                        evac(gi * 4 + si, self.ps[0:wd, base + si, 0:N])
            self.reserved = set()


def nat(c0, n):
    return [[(c0 + i * 128, 128)] for i in range(n)]


def build_program(NS, T, NB, DO_SAMPLE=True, DO_PROMPT=True):
    nc = bass.Bass("TRN2", target_bir_lowering=False)
    c = Ctx(nc)
    em = c.em
    NTL = T // 512

    def din(name, shape):
        return nc.dram_tensor(name, list(shape), F32, kind="ExternalInput").ap()

    def dout(name, shape):
        return nc.dram_tensor(name, list(shape), F32, kind="ExternalOutput").ap()

    XP = din("xp", [NS * T, D]); XS = din("xs", [NB, D])
    SG = din("sgdn", [NB * 8 * 128, 128]); SC = din("sconv", [NB * 3, 3072])
    CK = din("ck", [NB * 128, 256]); CV = din("cv", [NB * 128, 256])
    CMK = din("cmk", [NB * 256, 512]); CMV = din("cmv", [NB * 256, 512])
    MEMP = din("memp", [NS * 256, D])
    W = {}
    for nm, shp in [("ffn1_w1", [D, DFF]), ("ffn1_w3", [D, DFF]), ("ffn1_w2", [DFF, D]), ("w_in", [D, 9744]),
                    ("w_br_gdn", [1024, D]), ("w_br_swa", [1024, D]), ("w_out", [D, D]), ("w_mem_q", [D, 512]),
                    ("w_mem_k", [D, 512]), ("w_mem_v", [D, 512]), ("w_mem_o", [512, D]),
                    ("ffn2_w1", [D, DFF]), ("ffn2_w3", [D, DFF]), ("ffn2_w2", [DFF, D])]:
        W[nm] = din(nm, shp)
    GAINS = din("gains", [128, 96]); CONVW = din("convw", [128, 96])
    ALOG = din("alog", [8, 1]); DTB = din("dtb", [8, 1]); GN = din("gnorm", [128, 1]); SINKS = din("sinks", [1, 16])
    C_ID = din("c_ident", [128, 128]); C_ROT = din("c_rot", [128, 128])
    C_MS = din("c_mstrict4", [128, 512]); C_MI = din("c_minclT4", [128, 512]); C_ID4 = din("c_ident4", [128, 512])
    C_E8 = din("c_e8", [8, 1024]); C_E16 = din("c_e16", [16, 2048])
    C_COSP = din("c_cosp", [128, T]); C_SINP = din("c_sinp", [128, T])
    C_COSS = din("c_coss", [128, 1]); C_SINS = din("c_sins", [128, 1]); SINKC = din("sinks_col", [16, 1])

    YP = dout("y_p", [NS * T, D]); YS = dout("y_s", [NB, D])
    OGP = dout("o_gdn_p", [NS * 8 * 128, 128]); OCP = dout("o_conv_p", [NS * 3, 3072])
    OKP = dout("o_k_p", [NS * 128, 256]); OVP = dout("o_v_p", [NS * 128, 256])
    OMKP = dout("o_mk_p", [NS * 256, 512]); OMVP = dout("o_mv_p", [NS * 256, 512])
    OGS = dout("o_gdn_s", [NB * 8 * 128, 128]); OCS = dout("o_conv_s", [NB * 3, 3072])
    OKS = dout("o_k_s", [NB * 128, 256]); OVS = dout("o_v_s", [NB * 128, 256])

    SCR_H = 112
    SCR_N = 2 * SCR_H
    scr_ts = [nc.dram_tensor("wscratch%d" % j, [SCR_H, 128, 4096], BF16, kind="Internal").ap() for j in range(2)]
    c.scratch = [scr_ts[i // SCR_H][i % SCR_H].rearrange("p (k f) -> p k f", f=512) for i in range(SCR_N)]
    c.scratch_n = SCR_N
    st = contextlib.ExitStack()
    with st:
        def sb(name, shape, dt=F32):
            return st.enter_context(nc.sbuf_tensor(name, list(shape), dt))

        c.ps = st.enter_context(nc.psum_tensor("psum", [128, 8, 512], F32))
        ps = c.ps
        xres = sb("xres", [128, KC, 512]); xn = sb("xn", [128, KC, 512], BF16)
        c.wslots = [sb("ws%d" % i, [128, 8, 512], BF16) for i in range(6)]
        big = sb("big", [128, 8192])
        obr = sb("obr", [128, 8, 512], BF16); merged = sb("merged", [128, KC, 512], BF16)
        ident = sb("ident", [128, 128]); rot = sb("rot", [128, 128]); ones_f = sb("ones_f", [128, 128])
        ones_b = sb("ones_b", [128, 128], BF16)
        ms4b = sb("ms4b", [128, 512], BF16); mi4b = sb("mi4b", [128, 512], BF16); id4 = sb("id4b", [128, 512], BF16)
        ms4 = ms4b; mi4 = mi4b
        gains = sb("gains_s", [128, 96]); convw = sb("convw_s", [128, 96])
        alog = sb("alog_s", [8, 1]); dtb = sb("dtb_s", [8, 1]); gn = sb("gn_s", [128, 1]); sk = sb("sk_s", [1, 16])
        nega = sb("nega", [8, 1]); sinkcol = sb("sinkcol", [128, 16])
        rstd = sb("rstd", [128, 512]); sqt = sb("sqt", [128, 2, 512], BF16)
        ropes = sb("ropes", [128, 2])
        halo = sb("halo", [128, 24, 3]); Sst = sb("Sst", [128, 8, 128])
        kLo = sb("kLo", [128, 4, 640], BF16); kHi = sb("kHi", [128, 4, 640], BF16); vdup = sb("vdup", [128, 5, 4, 128], BF16)
        mkT = sb("mkT", [128, 4, 256], BF16); mvtok = sb("mvtok", [128, 2, 512], BF16)
        g8 = sb("g8", [8, 3, 512]); dg = sb("dg", [8, 256])
        colb = sb("colb", [128, 6, 32])
        ssq = sb("ssq", [128, 8]); junk = sb("junk", [128, 512]); tmpb = sqt
        mflat = merged[:, :, :].rearrange("p a n -> p (a n)").bitcast(F32)
        chn = mflat[:, 0:3072].rearrange("p (a n) -> p a n", n=512)
        e8 = mflat[0:8, 3072:4096]

        def bigv(off, n, dt=F32):
            a = big[:, off:off + n]
            return a.bitcast(dt) if dt != F32 else a

        for dst, src in [(ident, C_ID), (rot, C_ROT),
                         (gains, GAINS), (convw, CONVW), (alog, ALOG), (dtb, DTB), (gn, GN), (sk, SINKS),
                         (ropes[:, 0:1], C_COSS), (ropes[:, 1:2], C_SINS)]:
            d = dst[:] if not _is_ap(dst) else dst
            em.dma("sp", "c_" + d.tensor.name + str(d.offset), d, src, w=[d])
        c.memset("dve", ones_f[:], 1.0)
        c.memset("pool", kLo[:, :, :], 0.0)
        c.memset("pool", kHi[:, :, :], 0.0)
        c.cp("dve", ones_b[:], ones_f[:])
        for dstb, srcc in ((ms4b, C_MS), (mi4b, C_MI), (id4, C_ID4)):
            em.dma("sp", "cst_tmp", big[:, 0:512], srcc, w=[big[:, 0:512]])
            c.cp("dve", dstb[:], big[:, 0:512])
        c.act(nega[:], alog[:], AF.Exp)
        c.ts("dve", nega[:], nega[:], -1.0, ALU.mult)
        c.act(sk[:], sk[:], AF.Exp)
        b0 = c.bank()
        c.mm(b0[:, 0:16], ones_f[0:1, :], sk[0:1, :])
        c.cp("dve", sinkcol[:, 0:16], b0[:, 0:16])

        def norm(gi, N, out_fn=None):
            b = c.bank()
            for kc in range(KC):
                sq = sqt[:, kc % 2, 0:N]
                if kc % 2 == 0 or not int(_os0.environ.get("DBG_V4", "0")):
                    c.act(sq, xres[:, kc, 0:N], AF.Square)
                else:
                    c.tt("dve", sq, xres[:, kc, 0:N], xres[:, kc, 0:N], ALU.mult)
                c.mm(b[:, 0:N], ones_b[:], sq, start=(kc == 0), stop=(kc == KC - 1))
            rsq(rstd[:, 0:N], b[:, 0:N], 1.0 / D)
            for kc in range(KC):
                o = xn[:, kc, 0:N] if out_fn is None else out_fn(kc)
                c.stt(o, xres[:, kc, 0:N], gains[:, gi * 16 + kc:gi * 16 + kc + 1], rstd[:, 0:N], ALU.mult, ALU.mult)

        epsb = sb("epsb", [128, 4])
        c.memset("dve", epsb[:, 0:1], EPS)
        c.memset("dve", epsb[:, 1:2], float(np.log(128.0 ** -0.5)))
        c.memset("dve", epsb[:, 2:3], 1.0)
        c.memset("dve", epsb[:, 3:4], 0.0)

        def rsq(out, in_, scale, biascol=0, post_bias=None):
            c.act(out, in_, AF.Ln, scale=scale, bias=epsb[0:out.ap[0][1], biascol:biascol + 1])
            if post_bias is None:
                c.act(out, out, AF.Exp, scale=-0.5)
            else:
                c.act(out, out, AF.Exp, scale=-0.5, bias=post_bias)

        silu4 = sb("silu4", [128, 4, 512], BF16)

        def ffn(w1n, w3n, w2n, gi, N):
            norm(gi, N)
            silu_t = [silu4[:, m, 0:N] for m in range(4)]
            hb = bigv(0, 22 * 256, BF16).rearrange("p (c n) -> p c n", n=512)
            for half in range(2):
                f0 = half * 22
                for g0 in range(0, 22, 4):
                    ng = min(4, 22 - g0)
                    pc = [(0, (f0 + g0) * 128, ng * 128)]
                    for k0 in (0, 8):
                        w1 = c.load_w(W[w1n], k0, 8, pc)
                        w3 = c.load_w(W[w3n], k0, 8, pc)
                        for m in range(ng):
                            for kk in range(8):
                                c.mm(ps[:, m, 0:N], w1[kk][:, m * 128:(m + 1) * 128], xn[:, k0 + kk, 0:N], start=(k0 + kk == 0), stop=(k0 + kk == 15))
                            if k0 == 8:
                                c.act(silu_t[m], ps[:, m, 0:N], AF.Silu)
                        for m in range(ng):
                            for kk in range(8):
                                c.mm(ps[:, 4 + m, 0:N], w3[kk][:, m * 128:(m + 1) * 128], xn[:, k0 + kk, 0:N], start=(k0 + kk == 0), stop=(k0 + kk == 15))
                            if k0 == 8:
                                c.tt("dve", hb[:, g0 + m, 0:N], silu_t[m], ps[:, 4 + m, 0:N], ALU.mult)
                for og in range(4):
                    base = c.bankset()
                    for (k0, kn) in [(0, 8), (8, 8), (16, 6)]:
                        w2 = c.load_w(W[w2n], f0 + k0, kn, [(0, og * 512, 512)])
                        for m in range(4):
                            for kk in range(kn):
                                c.mm(ps[:, base + m, 0:N], w2[kk][:, m * 128:(m + 1) * 128], hb[:, k0 + kk, 0:N],
                                     start=(k0 + kk == 0), stop=(k0 + kk == 21))
                    for m in range(4):
                        xr = xres[:, og * 4 + m, 0:N]
                        c.stt(xr, ps[:, base + m, 0:N], 0.5, xr, ALU.mult, ALU.add)

        def branch(wbr, gate0, first, N):
            for g in range(4):
                wA = c.load_w(W[wbr], 0, 8, [(0, g * 512, 512)])
                wB = c.load_w(W["w_in"], 0, 16, [(0, gate0 + g * 512, 512)])
                for m in range(4):
                    for kk in range(8):
                        c.mm(ps[:, m, 0:N], wA[kk][:, m * 128:(m + 1) * 128], obr[:, kk, 0:N], start=(kk == 0), stop=(kk == 7))
                    for kk in range(16):
                        c.mm(ps[:, 4 + m, 0:N], wB[kk][:, m * 128:(m + 1) * 128], xn[:, kk, 0:N], start=(kk == 0), stop=(kk == 15))
                    t = tmpb[:, m % 2, 0:N]
                    c.act(t, ps[:, 4 + m, 0:N], AF.Sigmoid)
                    mg = merged[:, g * 4 + m, 0:N]
                    if first:
                        c.tt("dve", mg, t, ps[:, m, 0:N], ALU.mult)
                    else:
                        c.tt("dve", t, t, ps[:, m, 0:N], ALU.mult)
                        c.tt("dve", mg, mg, t, ALU.add)

        def add_res(N):
            def ev(i, p):
                xr = xres[:, i, 0:N]
                c.tt("dve", xr, xr, p, ALU.add)
            return ev

        def l2n(out, y, N, post_bias, slot=0):
            sq = sqt[:, slot, 0:N]
            c.act(sq, y, AF.Square)
            b = c.bank()
            c.mm(b[:, 0:N], ones_b[:], sq)
            r = (junk if slot == 0 else rstd)[:, 0:N]
            rsq(r, b[:, 0:N], 1.0, 0, post_bias)
            c.tt("dve", out, y, r, ALU.mult)

        def final_out(N, Ydst, tokmajor_rows):
            nb = max(1, N // 128)
            pw = min(N, 128)
            ystage = bigv(0, nb * 2048).rearrange("p (b d) -> p b d", d=2048)
            yT = junk
            b = c.bank()
            for kc in range(KC):
                sq = sqt[:, kc % 2, 0:N]
                if kc % 2 == 0 or not int(_os0.environ.get("DBG_V4", "0")):
                    c.act(sq, xres[:, kc, 0:N], AF.Square)
                else:
                    c.tt("dve", sq, xres[:, kc, 0:N], xres[:, kc, 0:N], ALU.mult)
                c.mm(b[:, 0:N], ones_b[:], sq, start=(kc == 0), stop=(kc == KC - 1))
            rsq(rstd[:, 0:N], b[:, 0:N], 1.0 / D)
            for kc in range(KC):
                c.stt(yT[:, 0:N], xres[:, kc, 0:N], gains[:, NORM_FINAL * 16 + kc:NORM_FINAL * 16 + kc + 1], rstd[:, 0:N], ALU.mult, ALU.mult)
                b2 = c.bank()
                for bl in range(nb):
                    c.tr(b2[0:pw, bl * 128:(bl + 1) * 128], yT[:, bl * 128:bl * 128 + pw], ident[:])
                c.cp("act", ystage[0:pw, :, kc * 128:(kc + 1) * 128], b2[0:pw, 0:nb * 128].rearrange("p (b d) -> p b d", d=128))
            if N >= 128:
                em.dma("sp", "ystage", Ydst.rearrange("(b p) d -> p b d", p=128), ystage[:, :, :], r=[ystage[:, :, :]], final=True)
            else:
                em.dma("sp", "ystage", Ydst, ystage[0:pw, 0, :], r=[ystage[0:pw, 0, :]], final=True)

        def load_x(Xsrc, N):
            nb = max(1, N // 128)
            pw = min(N, 128)
            xin = bigv(0, nb * 2048).rearrange("p (b d) -> p b d", d=2048)
            if N >= 128:
                em.dma("sp", "xin", xin[:, :, :], Xsrc.rearrange("(b p) d -> p b d", p=128), w=[xin[:, :, :]])
            else:
                em.dma("sp", "xin", xin[0:pw, 0, :], Xsrc, w=[xin[0:pw, 0, :]])
            for kc in range(KC):
                b = c.bank()
                for bl in range(nb):
                    c.tr(b[:, bl * 128:bl * 128 + pw], xin[0:pw, bl, kc * 128:(kc + 1) * 128], ident[0:pw, 0:pw])
                c.cp("act" if kc % 2 else "dve", xres[:, kc, 0:N], b[:, 0:N])

        xflat = xn[:, :, :].rearrange("p a n -> p (a n)").bitcast(F32)
        xinA = xflat.rearrange("p (b d) -> p b d", d=2048)
        xinB = mflat.rearrange("p (b d) -> p b d", d=2048)

        def issue_x(row0):
            em.dma("sp", "xinA", xinA, XP[row0:row0 + 256, :].rearrange("(b p) d -> p b d", p=128), w=[xinA])
            em.dma("sp", "xinB", xinB, XP[row0 + 256:row0 + 512, :].rearrange("(b p) d -> p b d", p=128), w=[xinB])

        def consume_x():
            for kc in range(KC):
                b = c.bank()
                for bl in range(4):
                    src = (xinA if bl < 2 else xinB)[:, bl % 2, kc * 128:(kc + 1) * 128]
                    c.tr(b[:, bl * 128:(bl + 1) * 128], src, ident[:])
                c.cp("act" if kc % 2 else "dve", xres[:, kc, 0:512], b[:, 0:512])

        def mem_kv(s):
            mst = bigv(0, 4096).rearrange("p (b d) -> p b d", d=2048)
            jb = bigv(4096, 2048)
            mstage = bigv(6144, 1024).rearrange("p (b d) -> p b d", d=512)
            em.dma("sp", "mst", mst[:, :, :], MEMP[s * 256:(s + 1) * 256, :].rearrange("(b p) d -> p b d", p=128), w=[mst[:, :, :]])
            for bl in range(2):
                c.act(jb, mst[:, bl, :], AF.Square)
                c.rsum(ssq[:, bl:bl + 1], jb)
            rsq(ssq[:, 2:4], ssq[:, 0:2], 1.0 / D)
            for bl in range(2):
                c.ts("dve", mst[:, bl, :], mst[:, bl, :], ssq[:, 2 + bl:3 + bl], ALU.mult)
            mh = xn
            _mk = int(_os.environ.get("DBG_MK", "9"))
            if _mk < 2:
                return
            for kc in range(KC):
                b = c.bank()
                for bl in range(2):
                    c.tr(b[:, bl * 128:(bl + 1) * 128], mst[:, bl, kc * 128:(kc + 1) * 128], ident[:])
                c.ts("dve", mh[:, kc, 0:256], b[:, 0:256], gains[:, NORM_MEMKV * 16 + kc:NORM_MEMKV * 16 + kc + 1], ALU.mult)
            if _mk < 3:
                return
            for wn, OUT, isv in (("w_mem_k", OMKP, False), ("w_mem_v", OMVP, True)):
                wt = c.load_w(W[wn], 0, 16, [(0, 0, 512)])
                if _mk < 4:
                    continue
                if not isv:
                    for m in range(4):
                        b = c.bank()
                        for kk in range(16):
                            c.mm(b[:, 0:256], wt[kk][:, m * 128:(m + 1) * 128], mh[:, kk, 0:256], start=(kk == 0), stop=(kk == 15))
                        c.cp("act", mkT[:, m, :], b[:, 0:256])
                if _mk < 5:
                    continue
                for bl in range(2):
                    b = c.bank()
                    for kk in range(16):
                        c.mm(b[:, :], mh[:, kk, bl * 128:(bl + 1) * 128], wt[kk], start=(kk == 0), stop=(kk == 15))
                    c.cp("act", mstage[:, bl, :], b[:, :])
                    if isv:
                        c.cp("dve", mvtok[:, bl, :], b[:, :])
                if _mk >= 6:
                    for bl in range(2):
                        em.dma("sp", "mstage%d" % bl, OUT[s * 256 + bl * 128:s * 256 + (bl + 1) * 128, :], mstage[:, bl, :],
                               r=[mstage[:, bl, :]], final=True)

        def conv_p(ch, p, cb, y, N):
            c.cp("act", cb[:, 0:3], halo[:, ch, :])
            c.cp("act", cb[:, 3:3 + N], p)
            c.cp("act", halo[:, ch, :], cb[:, N:N + 3])
            c.ts("dve", y, cb[:, 3:3 + N], convw[:, ch * 4 + 3:ch * 4 + 4], ALU.mult)
            for i in (2, 1, 0):
                c.stt(y, cb[:, i:i + N], convw[:, ch * 4 + i:ch * 4 + i + 1], y, ALU.mult, ALU.add)
            c.act(y, y, AF.Silu)

        def gdn_prompt(N):
            NBk = N // 128
            em.dma("sp", "e8", e8, C_E8[:, :], w=[e8])
            gcol, bcol, gamc, ngam, kdec, gend = [colb[:, i, :] for i in range(6)]
            bT, gT, GT = g8[:, 0, :], g8[:, 1, :], g8[:, 2, :]

            def ev_ba(i, p):
                if i == 0:
                    c.act(bT[:, 0:N], p, AF.Sigmoid)
                else:
                    c.act(gT[:, 0:N], p, AF.Exp, bias=dtb[:])
                    c.act(gT[:, 0:N], gT[:, 0:N], AF.Ln, bias=epsb[0:8, 2:3])
                    c.ts("dve", gT[:, 0:N], gT[:, 0:N], nega[:], ALU.mult)
            c.dense(W["w_in"], 16, [[(4096, 8)], [(4104, 8)]], lambda kc: xn[:, kc, 0:N], N, ev_ba)
            for bl in range(NBk):
                em.op("dve", lambda e, bl=bl: e.tensor_tensor_scan(out=GT[:, bl * 128:(bl + 1) * 128], data0=ones_f[0:8, 0:128],
                                                                    data1=gT[:, bl * 128:(bl + 1) * 128], initial=0.0,
                                                                    op0=ALU.mult, op1=ALU.add),
                      r=[gT[:, bl * 128:(bl + 1) * 128]], w=[GT[:, bl * 128:(bl + 1) * 128]])
            b = c.bank()
            for bl in range(NBk):
                c.tr(b[:, bl * 8:bl * 8 + 8], GT[:, bl * 128:(bl + 1) * 128], ident[0:8, 0:8])
                c.tr(b[:, 32 + bl * 8:32 + bl * 8 + 8], bT[:, bl * 128:(bl + 1) * 128], ident[0:8, 0:8])
            c.cp("dve", gcol[:, 0:NBk * 8], b[:, 0:NBk * 8])
            c.cp("dve", bcol[:, 0:NBk * 8], b[:, 32:32 + NBk * 8])
            c.act(gamc, gcol, AF.Exp)
            c.ts("dve", ngam, gamc, -1.0, ALU.mult)
            for bl in range(NBk):
                c.ts("dve", dg[:, bl * 8:bl * 8 + 8], ident[0:8, 0:8], GT[:, bl * 128 + 127:bl * 128 + 128], ALU.mult)
            b = c.bank()
            c.mm(b[:, 0:NBk * 8], ones_f[0:8, :], dg[:, 0:NBk * 8])
            c.cp("dve", gend, b[:, 0:32])
            c.tt("dve", kdec, gend, gcol, ALU.subtract)
            c.act(kdec, kdec, AF.Exp)
            c.act(gend, gend, AF.Exp)

            Grow = bigv(0, 512); gamrow = bigv(512, 512)
            cbq = bigv(1024, 520); cbk = bigv(1544, 520); cbv = bigv(2064, 520)
            yq = bigv(2584, 512); E = bigv(3096, 512); Dm = bigv(3608, 512)
            qT = bigv(4120, 256, BF16); qgT = bigv(4376, 256, BF16); kT = bigv(4632, 256, BF16)
            Kend = bigv(4888, 256, BF16); Vtok = bigv(5144, 256, BF16); MT = bigv(5400, 256, BF16)
            QKDT = bigv(5656, 256, BF16); szb = bigv(5912, 256, BF16)
            X = bigv(6168, 64, BF16); U = bigv(6232, 64, BF16); On = bigv(6296, 64, BF16); Sbf = bigv(6360, 64, BF16)
            P = [chn[:, 0, :], chn[:, 1, :]]; PT = [chn[:, 2, :], chn[:, 3, :]]; TT = [chn[:, 4, :], chn[:, 5, :]]
            idb = id4[:, 0:128]
            Af = mflat[:, 0:512]
            Pb = [mflat[:, 512:768].bitcast(BF16), mflat[:, 768:1024].bitcast(BF16)]
            PTb = [mflat[:, 1024:1280].bitcast(BF16), mflat[:, 1280:1536].bitcast(BF16)]
            TTb = [mflat[:, 1536:1792].bitcast(BF16), mflat[:, 1792:2048].bitcast(BF16)]
            Xf = mflat[:, 2048:2560]; Wf = mflat[:, 512:1024]; Tn = mflat[:, 1024:1536]

            def blk(a, bl):
                return a[:, bl * 128:(bl + 1) * 128]

            def head_dense(h):
                base = c.bankset()
                c.reserved = set(range(base, base + 4))
                cols = [h * 128, 1024 + h * 128, 2048 + h * 128, 3072 + h * 128]
                pieces = [(si * 128, cols[si], 128) for si in range(4)]
                for k0 in (0, 8):
                    wt = c.load_w(W["w_in"], k0, 8, pieces)
                    for si in range(4):
                        for kk in range(8):
                            c.mm(ps[:, base + si, 0:N], wt[kk][:, si * 128:(si + 1) * 128], xn[:, k0 + kk, 0:N],
                                 start=(k0 + kk == 0), stop=(k0 + kk == 15))
                return base

            def head_copies(h, base):
                for i, cbx in enumerate((cbq, cbk, cbv)):
                    ch = i * 8 + h
                    c.cp("act", cbx[:, 0:3], halo[:, ch, :])
                    c.cp("act" if i != 1 else "dve", cbx[:, 3:3 + N], ps[:, base + i, 0:N])
                    c.cp("act", halo[:, ch, :], cbx[:, N:N + 3])
                c.act(szb[:, 0:N], ps[:, base + 3, 0:N], AF.Silu)
                c.reserved = set()

            def conv_m(ch, cbx, y):
                c.ts("dve", y, cbx[:, 3:3 + N], convw[:, ch * 4 + 3:ch * 4 + 4], ALU.mult)
                for i in (2, 1, 0):
                    c.stt(y, cbx[:, i:i + N], convw[:, ch * 4 + i:ch * 4 + i + 1], y, ALU.mult, ALU.add)
                c.act(y, y, AF.Silu)

            V1 = int(_os.environ.get("DBG_V1", "1"))
            V2 = int(_os.environ.get("DBG_V2", "1"))
            yqs = [yq, bigv(6424, 512), bigv(6936, 512)] if V1 else [yq, yq, yq]
            X_ = [X, bigv(7448, 64, BF16)] if V2 else [X, X]
            U_ = [U, bigv(7512, 64, BF16)] if V2 else [U, U]
            On_ = [On, bigv(7576, 64, BF16)] if V2 else [On, On]

            def head_process(h):
                conv_m(h, cbq, yqs[0][:, 0:N])
                l2n(qT[:, 0:N], yqs[0][:, 0:N], N, epsb[:, 1:2], 0)
                conv_m(8 + h, cbk, yqs[1][:, 0:N])
                l2n(kT[:, 0:N], yqs[1][:, 0:N], N, None, 1 if V1 else 0)
                conv_m(16 + h, cbv, yqs[2][:, 0:N])
                b = c.bank()
                for bl in range(NBk):
                    c.tr(blk(b, bl), blk(yqs[2], bl), ident[:])
                c.cp("act", Vtok[:, 0:N], b[:, 0:N])

            hbase = head_dense(0)
            for h in range(8):
                head_copies(h, hbase)
                if h < 7:
                    hbase = head_dense(h + 1)
                head_process(h)
                Sh = Sst[:, h, :]
                c.cp("act", Sbf, Sh)
                b = c.bank()
                c.mm(b[:, 0:N], e8[:, h * 128:(h + 1) * 128], GT[:, 0:N])
                c.cp("act", Grow[:, 0:N], b[:, 0:N])
                c.act(gamrow[:, 0:N], Grow[:, 0:N], AF.Exp)
                c.tt("dve", qgT[:, 0:N], qT[:, 0:N], gamrow[:, 0:N], ALU.mult)
                bb = c.bank().bitcast(BF16)
                for bl in range(NBk):
                    c.tr(blk(bb, bl), blk(kT, bl), idb)
                for bl in range(NBk):
                    c.ts("dve", blk(Kend, bl), blk(bb, bl), kdec[:, bl * 8 + h:bl * 8 + h + 1], ALU.mult)
                bKK = c.bank()
                for bl in range(NBk):
                    c.mm(blk(bKK, bl), blk(kT, bl), blk(kT, bl))
                    c.ts("dve", blk(E, bl), blk(Grow, bl), gcol[:, bl * 8 + h:bl * 8 + h + 1], ALU.subtract, 0.0, ALU.max)
                c.act(E[:, 0:N], E[:, 0:N], AF.Exp, scale=-1.0)
                for bl in range(NBk):
                    c.stt(blk(Dm, bl), blk(E, bl), bcol[:, bl * 8 + h:bl * 8 + h + 1], blk(ms4, bl), ALU.mult, ALU.mult)
                c.tt("dve", Af[:, 0:N], bKK[:, 0:N], Dm[:, 0:N], ALU.mult)
                c.cp("act", Pb[0][:, 0:N], Af[:, 0:N])
                bb = c.bank().bitcast(BF16)
                for bl in range(NBk):
                    c.tr(blk(bb, bl), blk(Pb[0], bl), idb)
                c.cp("act", PTb[0][:, 0:N], bb[:, 0:N])
                c.tt("dve", TTb[0][:, 0:N], id4[:, 0:N], PTb[0][:, 0:N], ALU.subtract)
                a, ta = 0, 0
                for lvl in range(1, 7):
                    b1 = c.bank()
                    for bl in range(NBk):
                        c.mm(blk(b1, bl), blk(PTb[a], bl), blk(Pb[a], bl))
                    c.cp("act", Pb[1 - a][:, 0:N], b1[:, 0:N])
                    if lvl < 6:
                        b2 = c.bank()
                        for bl in range(NBk):
                            c.mm(blk(b2, bl), blk(Pb[a], bl), blk(PTb[a], bl))
                        c.cp("dve", PTb[1 - a][:, 0:N], b2[:, 0:N])
                    b3 = c.bank()
                    for bl in range(NBk):
                        c.mm(blk(b3, bl), blk(Pb[1 - a], bl), blk(TTb[ta], bl))
                    c.tt("dve", TTb[1 - ta][:, 0:N], b3[:, 0:N], TTb[ta][:, 0:N], ALU.add)
                    a = 1 - a
                    ta = 1 - ta
                Xb = TTb[ta]
                c.cp("act", Xf[:, 0:N], Xb[:, 0:N])
                bb = c.bank().bitcast(BF16)
                for bl in range(NBk):
                    c.tr(blk(bb, bl), blk(Xb, bl), idb)
                c.cp("dve", Tn[:, 0:N], bb[:, 0:N])
                bR = c.bank()
                for bl in range(NBk):
                    c.mm(blk(bR, bl), blk(Af, bl), blk(Xf, bl))
                c.stt(Wf[:, 0:N], bR[:, 0:N], -1.0, Xf[:, 0:N], ALU.mult, ALU.subtract)
                c.stt(Wf[:, 0:N], id4[:, 0:N], 2.0, Wf[:, 0:N], ALU.mult, ALU.add)
                bX = c.bank()
                for bl in range(NBk):
                    c.mm(blk(bX, bl), blk(Tn, bl), blk(Wf, bl))
                for bl in range(NBk):
                    c.ts("dve", blk(MT, bl), blk(bX, bl), bcol[:, bl * 8 + h:bl * 8 + h + 1], ALU.mult)
                bQ = c.bank()
                for bl in range(NBk):
                    c.mm(blk(bQ, bl), blk(kT, bl), blk(qT, bl))
                    c.ts("dve", blk(E, bl), blk(Grow, bl), gcol[:, bl * 8 + h:bl * 8 + h + 1], ALU.subtract, 0.0, ALU.min)
                c.act(E[:, 0:N], E[:, 0:N], AF.Exp)
                c.tt("dve", E[:, 0:N], E[:, 0:N], mi4[:, 0:N], ALU.mult)
                c.tt("dve", QKDT[:, 0:N], bQ[:, 0:N], E[:, 0:N], ALU.mult)
                for bl in range(NBk):
                    X, U, On = X_[bl % 2], U_[bl % 2], On_[bl % 2]
                    sqj = E[:, (bl % 2) * 128:(bl % 2) * 128 + 128] if V2 else junk[:, 0:128]
                    sc0 = (4 + 2 * (bl % 2)) if V2 else 0
                    b = c.bank()
                    c.mm(b[:, 0:128], blk(kT, bl), Sbf)
                    c.stt(X, b[:, 0:128], ngam[:, bl * 8 + h:bl * 8 + h + 1], blk(Vtok, bl), ALU.mult, ALU.add)
                    b = c.bank()
                    c.mm(b[:, 0:128], blk(MT, bl), X)
                    c.cp("act", U, b[:, 0:128])
                    bO = c.bank()
                    c.mm(bO[:, 0:128], blk(qgT, bl), Sbf, start=True, stop=False)
                    c.mm(bO[:, 0:128], blk(QKDT, bl), U, start=False, stop=True)
                    c.act(sqj, bO[:, 0:128], AF.Square)
                    c.rsum(ssq[:, sc0:sc0 + 1], sqj)
                    bK = c.bank()
                    c.mm(bK[:, 0:128], blk(Kend, bl), U)
                    c.stt(Sh, Sh, gend[:, bl * 8 + h:bl * 8 + h + 1], bK[:, 0:128], ALU.mult, ALU.add)
                    if bl < NBk - 1:
                        c.cp("act", Sbf, Sh)
                    rsq(ssq[:, sc0 + 1:sc0 + 2], ssq[:, sc0:sc0 + 1], 1.0 / 128)
                    c.ts("dve", On, bO[:, 0:128], ssq[:, sc0 + 1:sc0 + 2], ALU.mult)
                    bT2 = c.bank().bitcast(BF16)
                    c.tr(bT2[:, 0:128], On, idb)
                    c.stt(obr[:, h, bl * 128:(bl + 1) * 128], bT2[:, 0:128], gn[:, 0:1], szb[:, bl * 128:(bl + 1) * 128], ALU.mult, ALU.mult)

        def swa_prompt(s, ti, N):
            NBk = N // 128
            t0 = ti * 512
            qsT = bigv(0, 2048, BF16).rearrange("p (c n) -> p c n", n=512)
            raw = bigv(2048, 512); t1 = bigv(2560, 512); krot = bigv(3072, 2048).rearrange("p (c n) -> p c n", n=512)
            vTf = bigv(5120, 1024).rearrange("p (c n) -> p c n", n=512)
            pT = bigv(6144, 512, BF16).rearrange("p (c n) -> p c n", n=512)
            rec = rstd; ropec = bigv(6656, 1024).rearrange("p (c n) -> p c n", n=512); stg = bigv(7680, 256)
            last = (ti == NTL - 1)
            em.dma("sp", "ropec", ropec[:, 0, 0:N], C_COSP[:, t0:t0 + N], w=[ropec[:, 0, 0:N]])
            em.dma("sp", "ropec", ropec[:, 1, 0:N], C_SINP[:, t0:t0 + N], w=[ropec[:, 1, 0:N]])
            em.restamp("ropec", [ropec[:, :, 0:N]])

            def ev(i, p):
                if i < 12:
                    c.cp("act", raw[:, 0:N], p)
                    b = c.bank()
                    c.mm(b[:, 0:N], rot[:], raw[:, 0:N])
                    c.tt("dve", t1[:, 0:N], raw[:, 0:N], ropec[:, 0, 0:N], ALU.mult)
                    c.tt("dve", raw[:, 0:N], b[:, 0:N], ropec[:, 1, 0:N], ALU.mult)
                    if i < 8:
                        c.tt("dve", qsT[:, i, 0:N], raw[:, 0:N], t1[:, 0:N], ALU.add)
                    else:
                        g = i - 8
                        c.tt("dve", krot[:, g, 0:N], raw[:, 0:N], t1[:, 0:N], ALU.add)
                        c.cp("act", kLo[0:64, g, 128:128 + N], krot[0:64, g, 0:N])
                        c.cp("act", kHi[64:128, g, 128:128 + N], krot[64:128, g, 0:N])
                else:
                    cc = i - 12
                    c.cp("act", vTf[:, cc, 0:N], p)
                    b = c.bank()
                    for bl in range(NBk):
                        c.tr(b[:, bl * 128:(bl + 1) * 128], vTf[:, cc, bl * 128:(bl + 1) * 128], ident[:])
                    bv = b[:, 0:N].rearrange("p (b e d) -> p b e d", e=2, d=64)
                    for dup in range(2):
                        c.cp("dve" if dup else "act", vdup[:, 1:1 + NBk, 2 * cc:2 * cc + 2, dup * 64:(dup + 1) * 64], bv)
                    if last:
                        c.cp("act", stg[:, cc * 128:(cc + 1) * 128], b[:, (NBk - 1) * 128:NBk * 128])
            subs = nat(SW_Q0, 8) + [[(SW_K0 + 64 * g, 64), (SW_K0 + 64 * g, 64)] for g in range(4)] + nat(SW_V0, 2)
            _sw = int(_os.environ.get("DBG_SWA", "9"))
            c.dense(W["w_in"], 16, subs, lambda kc: xn[:, kc, 0:N], N, ev)
            if _sw < 2:
                return
            if last:
                em.dma("sp", "stg", OVP[s * 128:(s + 1) * 128, :], stg[:, :], r=[stg[:, :]], final=True)
                stk = bigv(7936, 256)
                for g in range(4):
                    b = c.bank()
                    c.tr(b[:, 0:128], krot[:, g, N - 128:N], ident[:])
                    c.cp("act", stk[:, g * 64:(g + 1) * 64], b[:, 0:64])
                em.dma("sp", "stk", OKP[s * 128:(s + 1) * 128, :], stk[:, :], r=[stk[:, :]], final=True)
            pTb = [bigv(2048, 512, BF16).rearrange("p (c n) -> p c n", n=512), bigv(2560, 512, BF16).rearrange("p (c n) -> p c n", n=512)]
            its = [(g, qb) for g in range(4) for qb in range(NBk)]

            def kbs_of(qb):
                return [(qb, False)] if (ti == 0 and qb == 0) else [(qb - 1, True), (qb, False)]

            def stageA(i):
                g, qb = its[i]
                for idx, (kb, prev) in enumerate(kbs_of(qb)):
                    bs = c.bank()
                    kcols = slice(128 + kb * 128, 128 + (kb + 1) * 128)
                    for e_ in range(2):
                        c.mm(bs[:, e_ * 256:(e_ + 1) * 256], (kHi if e_ else kLo)[:, g, kcols],
                             qsT[:, 2 * g:2 * g + 2, qb * 128:(qb + 1) * 128])
                    pt = pTb[i % 2][:, idx, :]
                    c.act(pt, bs[:, :], AF.Exp, scale=0.125)
                    c.tt("dve", pt, pt, (ms4b if prev else mi4b)[:, :], ALU.mult)

            def stageB(i):
                g, qb = its[i]
                rec = (rstd if i % 2 == 0 else junk) if int(_os0.environ.get("DBG_V5", "1")) else rstd
                bo = c.bank()
                bd = c.bank()
                kbs = kbs_of(qb)
                for idx, (kb, prev) in enumerate(kbs):
                    pt = pTb[i % 2][:, idx, :]
                    c.mm(bo[:, :], vdup[:, 1 + kb, g, :], pt, start=(idx == 0), stop=(idx == len(kbs) - 1))
                    c.mm(bd[:, :], ones_b[:], pt, start=(idx == 0), stop=(idx == len(kbs) - 1))
                for e_ in range(2):
                    for cc in range(2):
                        hh = 4 * g + 2 * cc + e_
                        col = slice((e_ * 2 + cc) * 128, (e_ * 2 + cc + 1) * 128)
                        c.ts("dve", rec[:, col], bd[:, col], sinkcol[:, hh:hh + 1], ALU.add)
                c.act(rec[:, :], rec[:, :], AF.Ln)
                c.act(rec[:, :], rec[:, :], AF.Exp, scale=-1.0)
                for e_ in range(2):
                    pr = slice(e_ * 64, (e_ + 1) * 64)
                    c.tt("dve", obr[pr, 2 * g:2 * g + 2, qb * 128:(qb + 1) * 128],
                         bo[pr, e_ * 256:(e_ + 1) * 256].rearrange("p (c n) -> p c n", n=128),
                         rec[pr, e_ * 256:(e_ + 1) * 256].rearrange("p (c n) -> p c n", n=128), ALU.mult)

            stageA(0)
            for i in range(len(its)):
                if i + 1 < len(its):
                    stageA(i + 1)
                stageB(i)
            c.cp("act", kLo[0:64, :, 0:128], kLo[0:64, :, N:N + 128])
            c.cp("act", kHi[64:128, :, 0:128], kHi[64:128, :, N:N + 128])
            c.cp("dve", vdup[:, 0, :, :], vdup[:, NBk, :, :])

        mqT = bigv(1024, 1024, BF16).rearrange("p (c n) -> p c n", n=512)
        moT = bigv(2048, 1024, BF16).rearrange("p (c n) -> p c n", n=512)

        def memattn_prompt(N):
            norm(NORM_MEMQ, N)

            def ev(i, p):
                c.cp("act", mqT[:, i, 0:N], p)
            c.dense(W["w_mem_q"], 16, nat(0, 4), lambda kc: xn[:, kc, 0:N], N, ev)
            pT = bigv(0, 512, BF16).rearrange("p (c n) -> p c n", n=512)
            rec = bigv(512, 512)
            recs = [rec, bigv(3072, 512)] if int(_os0.environ.get("DBG_V6", "0")) else [rec, rec]
            pTs2 = [pT, bigv(3584, 512, BF16).rearrange("p (c n) -> p c n", n=512)] if int(_os0.environ.get("DBG_V6", "0")) else [pT, pT]
            for h in range(4):
                rec = recs[h % 2]
                pT = pTs2[h % 2]
                bo = c.bank()
                bd = c.bank()
                for mc in range(2):
                    bs = c.bank()
                    c.mm(bs[:, 0:N], mkT[:, h, mc * 128:(mc + 1) * 128], mqT[:, h, 0:N])
                    c.act(pT[:, mc, 0:N], bs[:, 0:N], AF.Exp, scale=128.0 ** -0.5)
                    c.mm(bo[:, 0:N], mvtok[:, mc, h * 128:(h + 1) * 128], pT[:, mc, 0:N], start=(mc == 0), stop=(mc == 1))
                    c.mm(bd[:, 0:N], ones_b[:], pT[:, mc, 0:N], start=(mc == 0), stop=(mc == 1))
                c.act(rec[:, 0:N], bd[:, 0:N], AF.Ln)
                c.act(rec[:, 0:N], rec[:, 0:N], AF.Exp, scale=-1.0)
                c.tt("dve", moT[:, h, 0:N], bo[:, 0:N], rec[:, 0:N], ALU.mult)
            c.dense(W["w_mem_o"], 4, nat(0, 16), lambda kc: moT[:, kc, 0:N], N, add_res(N))

        def prompt_tile(s, ti):
            N = 512
            row0 = s * T + ti * 512
            import os
            stop = int(os.environ.get("DBG_STOP", "99"))
            if ti == 0:
                issue_x(row0)
            consume_x()
            if stop >= 1:
                ffn("ffn1_w1", "ffn1_w3", "ffn1_w2", NORM_FFN1, N)
            if stop >= 2:
                norm(NORM_MIX, N)
                gdn_prompt(N)
            if stop >= 3:
                branch("w_br_gdn", GG0, True, N)
            if stop >= 4:
                swa_prompt(s, ti, N)
            if stop >= 5:
                branch("w_br_swa", GS0, False, N)
                c.dense(W["w_out"], 16, nat(0, 16), lambda kc: merged[:, kc, 0:N], N, add_res(N))
            if stop >= 6:
                memattn_prompt(N)
            if stop >= 7:
                ffn("ffn2_w1", "ffn2_w3", "ffn2_w2", NORM_FFN2, N)
            if ti + 1 < NTL:
                issue_x(row0 + 512)
            final_out(N, YP[row0:row0 + N, :], True)

        def prompt_seq_end(s):
            em.dma("sp", "Sst", OGP[s * 1024:(s + 1) * 1024, :].rearrange("(h p) v -> p h v", p=128), Sst[:, :, :],
                   r=[Sst[:, :, :]], final=True)
            cst = bigv(0, 3072)
            for q4 in range(6):
                b = c.bank()
                for j in range(4):
                    ch = q4 * 4 + j
                    c.tr(b[0:3, j * 128:(j + 1) * 128], halo[:, ch, :], ident[:])
                c.cp("act", cst[0:3, q4 * 512:(q4 + 1) * 512], b[0:3, :])
            em.dma("sp", "cst", OCP[s * 3:(s + 1) * 3, :], cst[0:3, :], r=[cst[0:3, :]], final=True)

        import os as _os
        _skip = _os.environ.get("DBG_SKIP", "")
        for s in range(NS if DO_PROMPT else 0):
            if "m" not in _skip:
                mem_kv(s)
            c.memset("dve", halo[:, :, :], 0.0)
            c.memset("dve", Sst[:, :, :], 0.0)
            for ti in range(NTL):
                if "t" not in _skip:
                    prompt_tile(s, ti)
            if "e" not in _skip:
                prompt_seq_end(s)


        AX = mybir.AxisListType

        def gdn_sample(N):
            chf = mflat[:, 0:3072]
            Sb = [chf[:, 0:1024].rearrange("p (h v) -> p h v", v=128), chf[:, 1024:2048].rearrange("p (h v) -> p h v", v=128)]
            Sout = chf[:, 2048:3072].rearrange("p (h v) -> p h v", v=128)
            scin = bigv(0, 3072)
            cstT = bigv(3072, 1152).rearrange("p (c n) -> p c n", n=48)
            qT8 = bigv(4224, 128).rearrange("p (h b) -> p h b", b=16); kT8 = bigv(4352, 128).rearrange("p (h b) -> p h b", b=16)
            vT8 = bigv(4480, 128).rearrange("p (h b) -> p h b", b=16); sz8 = bigv(4608, 128).rearrange("p (h b) -> p h b", b=16)
            kb8 = bigv(4736, 128).rearrange("p (h b) -> p h b", b=16)
            pnew = bigv(4864, 384).rearrange("p (c b) -> p c b", b=16)
            bcg = bigv(5248, 128); bcb = bigv(5376, 128); ngb = bigv(5504, 128)
            vtok = bigv(5632, 1024); vbc = bigv(6656, 1024)
            kbc_ = [bigv(7680, 128), bigv(2048, 128)]; gS_ = [bigv(7808, 128), bigv(2176, 128)]; up_ = [bigv(7936, 128), bigv(2304, 128)]
            yq = bigv(8064, 16); e16 = bigv(0, 2048)
            Sout_ = [Sout, mflat[:, 3072:4096].rearrange("p (h v) -> p h v", v=128)]
            bT, gT, gam = g8[:, 0, 0:N], g8[:, 1, 0:N], g8[:, 2, 0:N]
            em.dma("sp", "scin", scin[0:48, :], SC[:, :], w=[scin[0:48, :]])
            em.dma("sp", "ocs_a", OCS.rearrange("(b j) c -> b j c", j=3)[:, 0:2, :], SC.rearrange("(b j) c -> b j c", j=3)[:, 1:3, :], final=True)
            for c0 in range(0, 24, 8):
                b = c.bank()
                for j in range(8):
                    c.tr(b[:, j * 48:(j + 1) * 48], scin[0:48, (c0 + j) * 128:(c0 + j + 1) * 128], ident[0:48, 0:48])
                c.cp("act", cstT[:, c0:c0 + 8, :], b[:, 0:384].rearrange("p (c n) -> p c n", n=48))

            def ev_ba(i, p):
                if i == 0:
                    c.act(bT, p, AF.Sigmoid)
                else:
                    c.act(gT, p, AF.Exp, bias=dtb[:])
                    c.act(gT, gT, AF.Ln, bias=epsb[0:8, 2:3])
                    c.ts("dve", gT, gT, nega[:], ALU.mult)
                    c.act(gam, gT, AF.Exp)
            c.dense(W["w_in"], 16, [[(4096, 8)], [(4104, 8)]], lambda kc: xn[:, kc, 0:N], N, ev_ba)
            for hh in range(8):
                c.ts("dve", dg[:, hh * 16:(hh + 1) * 16], gam, ident[0:8, hh:hh + 1], ALU.mult)
                c.ts("dve", dg[:, 128 + hh * 16:128 + (hh + 1) * 16], bT, ident[0:8, hh:hh + 1], ALU.mult)
            b = c.bank()
            c.mm(b[:, 0:256], ones_f[0:8, :], dg[:, 0:256])
            c.cp("dve", bcg, b[:, 0:128]); c.cp("dve", bcb, b[:, 128:256])
            c.ts("dve", ngb, bcg, -1.0, ALU.mult)

            def conv_s(ch, p):
                c.cp("act", pnew[:, ch, :], p)
                cv = cstT[:, ch, :].rearrange("p (b j) -> p b j", j=3)
                c.ts("dve", yq, p, convw[:, ch * 4 + 3:ch * 4 + 4], ALU.mult)
                for j in (2, 1, 0):
                    c.stt(yq, cv[:, :, j], convw[:, ch * 4 + j:ch * 4 + j + 1], yq, ALU.mult, ALU.add)
                c.act(yq, yq, AF.Silu)

            for h in range(8):
                def ev(i, p, h=h):
                    if i == 0:
                        conv_s(h, p); l2n(qT8[:, h, :], yq, N, epsb[:, 1:2])
                    elif i == 1:
                        conv_s(8 + h, p); l2n(kT8[:, h, :], yq, N, None)
                        c.tt("dve", kb8[:, h, :], kT8[:, h, :], bcb[:, h * 16:(h + 1) * 16], ALU.mult)
                    elif i == 2:
                        conv_s(16 + h, p); c.cp("act", vT8[:, h, :], yq)
                    else:
                        c.act(sz8[:, h, :], p, AF.Silu)
                subs = [[(h * 128, 128)], [(1024 + h * 128, 128)], [(2048 + h * 128, 128)], [(3072 + h * 128, 128)]]
                c.dense(W["w_in"], 16, subs, lambda kc: xn[:, kc, 0:N], N, ev)
            cstg = scin
            for q4 in range(6):
                b = c.bank()
                for j in range(4):
                    c.tr(b[0:16, j * 128:(j + 1) * 128], pnew[:, q4 * 4 + j, :], ident[:])
                c.cp("act", cstg[0:16, q4 * 512:(q4 + 1) * 512], b[0:16, :])
            em.dma("sp", "scin", OCS.rearrange("(b j) c -> b j c", j=3)[:, 2, :], cstg[0:16, :], r=[cstg[0:16, :]], final=True)
            for hf in range(2):
                b = c.bank()
                for j in range(4):
                    c.tr(b[0:16, j * 128:(j + 1) * 128], vT8[:, hf * 4 + j, :], ident[:])
                c.cp("act", vtok[0:16, hf * 512:(hf + 1) * 512], b[0:16, :])
            em.dma("sp", "e16", e16[0:16, :], C_E16[:, :], w=[e16[0:16, :]])
            bo = ps[:, 7, :]
            c.reserved = {7}
            for bb in range(NB):
                S = Sb[bb % 2]
                em.dma("sp", "Sb%d" % (bb % 2), S, SG[bb * 1024:(bb + 1) * 1024, :].rearrange("(h p) v -> p h v", p=128), w=[S])
                for hf in range(2):
                    b = c.bank()
                    c.mm(b[:, :], e16[0:16, bb * 128:(bb + 1) * 128], vtok[0:16, hf * 512:(hf + 1) * 512])
                    c.cp("act", vbc[:, hf * 512:(hf + 1) * 512], b[:, :])
                Sout = Sout_[bb % 2]
                for h in range(8):
                    kbc, gS, up = kbc_[h % 2], gS_[h % 2], up_[h % 2]
                    col = slice(h * 16 + bb, h * 16 + bb + 1)
                    c.ts("dve", kbc, ones_f[:], kT8[:, h, bb:bb + 1], ALU.mult)
                    bk = c.bank()
                    c.mm(bk[:, 0:128], kbc, S[:, h, :])
                    c.stt(up, bk[:, 0:128], ngb[:, col], vbc[:, h * 128:(h + 1) * 128], ALU.mult, ALU.add)
                    c.act(gS, S[:, h, :], AF.Copy, scale=bcg[:, col])
                    c.stt(Sout[:, h, :], up, kb8[:, h, bb:bb + 1], gS, ALU.mult, ALU.add)
                    c.mm(bo[:, col], Sout[:, h, :], qT8[:, h, bb:bb + 1])
                em.dma("sp", "Sout%d" % (bb % 2), OGS[bb * 1024:(bb + 1) * 1024, :].rearrange("(h p) v -> p h v", p=128), Sout, r=[Sout], final=True)
            c.reserved = set()
            sq = sqt[:, 0, 0:128]
            c.act(sq, bo[:, 0:128], AF.Square)
            b = c.bank()
            c.mm(b[:, 0:128], ones_b[:], sq)
            rsq(junk[:, 0:128], b[:, 0:128], 1.0 / 128)
            c.tt("dve", junk[:, 0:128], bo[:, 0:128], junk[:, 0:128], ALU.mult)
            c.stt(obr[:, :, 0:16], junk[:, 0:128].rearrange("p (h b) -> p h b", b=16), gn[:, 0:1], sz8, ALU.mult, ALU.mult)

        def swa_sample(N):
            qs8 = bigv(0, 128).rearrange("p (c b) -> p c b", b=16); kd8 = bigv(128, 64).rearrange("p (c b) -> p c b", b=16)
            vT2 = bigv(192, 32).rearrange("p (c b) -> p c b", b=16); raw = bigv(224, 16); t1 = bigv(240, 16)
            knew = bigv(256, 256); vnew = bigv(512, 256)
            qz = bigv(768, 1024).rearrange("p (b g h) -> p b g h", g=4, h=16)
            Kf = [bigv(1792, 256), bigv(2048, 256)]; Vf = [bigv(2304, 256), bigv(2560, 256)]
            Kd_ = [bigv(2816, 512).rearrange("p (g d) -> p g d", d=128), bigv(4608, 512).rearrange("p (g d) -> p g d", d=128)]
            Vd_ = [bigv(3328, 512).rearrange("p (g d) -> p g d", d=128), bigv(5120, 512).rearrange("p (g d) -> p g d", d=128)]
            KTd_ = [bigv(3840, 512).rearrange("p (g k) -> p g k", k=128), bigv(5632, 512).rearrange("p (g k) -> p g k", k=128)]
            pr__ = [bigv(4352, 128), bigv(6144, 128)]; pTs_ = [bigv(4480, 16), bigv(6272, 16)]; st4_ = [bigv(4496, 8), bigv(6288, 8)]
            sinkc = bigv(4504, 1)

            def ev(i, p):
                if i < 14 and i < 12:
                    c.cp("act", raw, p)
                    b = c.bank()
                    c.mm(b[:, 0:N], rot[:], raw)
                    c.ts("dve", t1, raw, ropes[:, 0:1], ALU.mult)
                    dst = qs8[:, i, :] if i < 8 else kd8[:, i - 8, :]
                    c.stt(dst, b[:, 0:N], ropes[:, 1:2], t1, ALU.mult, ALU.add)
                else:
                    c.cp("act", vT2[:, i - 12, :], p)
            subs = nat(SW_Q0, 8) + [[(SW_K0 + 64 * g, 64), (SW_K0 + 64 * g, 64)] for g in range(4)] + nat(SW_V0, 2)
            c.dense(W["w_in"], 16, subs, lambda kc: xn[:, kc, 0:N], N, ev)
            b = c.bank()
            for g in range(4):
                c.tr(b[0:16, g * 128:(g + 1) * 128], kd8[:, g, :], ident[:])
            c.cp("act", knew[0:16, :].rearrange("p (g d) -> p g d", d=64), b[0:16, :].rearrange("p (g d) -> p g d", d=128)[:, :, 0:64])
            b = c.bank()
            for cc in range(2):
                c.tr(b[0:16, cc * 128:(cc + 1) * 128], vT2[:, cc, :], ident[:])
            c.cp("act", vnew[0:16, :], b[0:16, 0:256])
            for OUT, CIN, new, tok in ((OKS, CK, knew, "OKS"), (OVS, CV, vnew, "OVS")):
                o3 = OUT.rearrange("(b t) d -> b t d", t=128)
                em.dma("sp", "d_" + tok, o3[:, 0:127, :], CIN.rearrange("(b t) d -> b t d", t=128)[:, 1:128, :], tw=[tok + "a"], final=True)
                em.dma("sp", "n_" + tok, o3[:, 127, :], new[0:16, :], r=[new[0:16, :]], tw=[tok + "b"], final=True)
            em.dma("sp", "sinkc", sinkc[0:16, :], SINKC[:, :], w=[sinkc[0:16, :]])
            c.memset("dve", qz, 0.0)
            for g in range(4):
                for j in range(4):
                    e_ = j % 2
                    c.cp("act", qz[e_ * 64:(e_ + 1) * 64, :, g, 4 * g + j], qs8[e_ * 64:(e_ + 1) * 64, 2 * g + j // 2, :])
            for bb in range(NB):
                kf, vf = Kf[bb % 2], Vf[bb % 2]
                Kd, Vd, KTd, pr_, pTs, st4 = Kd_[bb % 2], Vd_[bb % 2], KTd_[bb % 2], pr__[bb % 2], pTs_[bb % 2], st4_[bb % 2]
                em.dma("sp", "kf%d" % (bb % 2), kf, OKS[bb * 128:(bb + 1) * 128, :], w=[kf], tr=["OKSa", "OKSb"])
                em.dma("sp", "vf%d" % (bb % 2), vf, OVS[bb * 128:(bb + 1) * 128, :], w=[vf], tr=["OVSa", "OVSb"])
                for dup in range(2):
                    c.cp("act", Kd[:, :, dup * 64:(dup + 1) * 64], kf.rearrange("p (g d) -> p g d", d=64))
                    c.cp("dve", Vd[:, :, dup * 64:(dup + 1) * 64], vf.rearrange("p (g d) -> p g d", d=64))
                b = c.bank()
                for g in range(4):
                    c.tr(b[:, g * 128:(g + 1) * 128], Kd[:, g, :], ident[:])
                c.cp("act", KTd, b[:, :].rearrange("p (g k) -> p g k", k=128))
                bs = c.bank()
                for g in range(4):
                    c.mm(bs[0:16, 0:128], qz[:, bb, g, :], KTd[:, g, :], start=(g == 0), stop=(g == 3))
                mx, ngm, sm, es = st4[0:16, 0:1], st4[0:16, 1:2], st4[0:16, 2:3], st4[0:16, 3:4]
                em.op("dve", lambda e, mx=mx, bs=bs: e.reduce_max(out=mx, in_=bs[0:16, 0:128], axis=AX.X), r=[bs[0:16, 0:128]], w=[mx])
                c.ts("dve", mx, mx, 0.125, ALU.mult, sinkc[0:16, :], ALU.max)
                c.ts("dve", ngm, mx, -1.0, ALU.mult)
                c.act(pr_[0:16, :], bs[0:16, 0:128], AF.Exp, scale=0.125, bias=ngm)
                c.rsum(sm, pr_[0:16, :])
                c.act(es, sinkc[0:16, :], AF.Exp, bias=ngm)
                c.tt("dve", sm, sm, es, ALU.add)
                em.op("dve", lambda e, sm=sm: e.reciprocal(out=sm, in_=sm), r=[sm], w=[sm])
                c.ts("dve", pr_[0:16, :], pr_[0:16, :], sm, ALU.mult)
                b = c.bank()
                c.tr(b[:, 0:16], pr_[0:16, :], ident[0:16, 0:16])
                c.cp("act", pTs, b[:, 0:16])
                bo = c.bank()
                for g in range(4):
                    c.mm(bo[:, g * 16:(g + 1) * 16], Vd[:, g, :], pTs)
                for g in range(4):
                    for e_ in range(2):
                        prt = slice(e_ * 64, (e_ + 1) * 64)
                        c.cp("dve" if e_ else "act", obr[prt, 2 * g:2 * g + 2, bb],
                             bo[prt, g * 16 + 4 * g + e_:g * 16 + 4 * g + e_ + 3:2])

        def memattn_sample(N):
            norm(NORM_MEMQ, N)
            mq8 = bigv(4096, 64).rearrange("p (h b) -> p h b", b=16)
            qz4 = bigv(4160, 256).rearrange("p (b h k) -> p b h k", h=4, k=4)
            Kc = [bigv(0, 1024).rearrange("p (c d) -> p c d", d=512), bigv(1024, 1024).rearrange("p (c d) -> p c d", d=512)]
            Vc = [bigv(2048, 1024).rearrange("p (c d) -> p c d", d=512), bigv(3072, 1024).rearrange("p (c d) -> p c d", d=512)]
            KTm_ = [bigv(4416, 1024).rearrange("p (h m) -> p h m", m=256), bigv(5760, 1024).rearrange("p (h m) -> p h m", m=256)]
            pm_ = [bigv(5440, 256), bigv(6784, 256)]
            pmT_ = [bigv(5696, 8).rearrange("p (c h) -> p c h", h=4), bigv(7040, 8).rearrange("p (c h) -> p c h", h=4)]
            st4_ = [bigv(5704, 8), bigv(7048, 8)]

            def ev(i, p):
                c.cp("act", mq8[:, i, :], p)
            c.dense(W["w_mem_q"], 16, nat(0, 4), lambda kc: xn[:, kc, 0:N], N, ev)
            c.memset("dve", qz4, 0.0)
            for h in range(4):
                c.cp("act", qz4[:, :, h, h], mq8[:, h, :])
            bo = ps[:, 7, :]
            c.reserved = {7}
            for bb in range(NB):
                kc_, vc_ = Kc[bb % 2], Vc[bb % 2]
                KTm, pm, pmT, st4 = KTm_[bb % 2], pm_[bb % 2], pmT_[bb % 2], st4_[bb % 2]
                em.dma("sp", "kc%d" % (bb % 2), kc_, CMK[bb * 256:(bb + 1) * 256, :].rearrange("(c p) d -> p c d", p=128), w=[kc_])
                em.dma("sp", "vc%d" % (bb % 2), vc_, CMV[bb * 256:(bb + 1) * 256, :].rearrange("(c p) d -> p c d", p=128), w=[vc_])
                for h in range(4):
                    b = c.bank()
                    for mc in range(2):
                        c.tr(b[:, mc * 128:(mc + 1) * 128], kc_[:, mc, h * 128:(h + 1) * 128], ident[:])
                    c.cp("act" if h % 2 else "dve", KTm[:, h, :], b[:, 0:256])
                bs = c.bank()
                for h in range(4):
                    c.mm(bs[0:4, 0:256], qz4[:, bb, h, :], KTm[:, h, :], start=(h == 0), stop=(h == 3))
                mx, ngm, sm = st4[0:4, 0:1], st4[0:4, 1:2], st4[0:4, 2:3]
                sc_ = 128.0 ** -0.5
                em.op("dve", lambda e, mx=mx, bs=bs: e.reduce_max(out=mx, in_=bs[0:4, 0:256], axis=AX.X), r=[bs[0:4, 0:256]], w=[mx])
                c.ts("dve", ngm, mx, -sc_, ALU.mult)
                c.act(pm[0:4, :], bs[0:4, 0:256], AF.Exp, scale=sc_, bias=ngm)
                c.rsum(sm, pm[0:4, :])
                em.op("dve", lambda e, sm=sm: e.reciprocal(out=sm, in_=sm), r=[sm], w=[sm])
                c.ts("dve", pm[0:4, :], pm[0:4, :], sm, ALU.mult)
                b = c.bank()
                for mc in range(2):
                    c.tr(b[:, mc * 4:(mc + 1) * 4], pm[0:4, mc * 128:(mc + 1) * 128], ident[0:4, 0:4])
                c.cp("act", pmT, b[:, 0:8].rearrange("p (c h) -> p c h", h=4))
                for h in range(4):
                    for mc in range(2):
                        c.mm(bo[:, h * 16 + bb:h * 16 + bb + 1], vc_[:, mc, h * 128:(h + 1) * 128], pmT[:, mc, h:h + 1],
                             start=(mc == 0), stop=(mc == 1))
            c.reserved = set()
            c.cp("act", moT[:, :, 0:16], bo[:, 0:64].rearrange("p (h b) -> p h b", b=16))
            c.dense(W["w_mem_o"], 4, nat(0, 16), lambda kc: moT[:, kc, 0:N], N, add_res(N))

        def sample_pass():
            N = NB
            load_x(XS[:, :], N)
            ffn("ffn1_w1", "ffn1_w3", "ffn1_w2", NORM_FFN1, N)
            norm(NORM_MIX, N)
            gdn_sample(N)
            branch("w_br_gdn", GG0, True, N)
            swa_sample(N)
            branch("w_br_swa", GS0, False, N)
            c.dense(W["w_out"], 16, nat(0, 16), lambda kc: merged[:, kc, 0:N], N, add_res(N))
            memattn_sample(N)
            ffn("ffn2_w1", "ffn2_w3", "ffn2_w2", NORM_FFN2, N)
            final_out(N, YS[:, :], True)

        if DO_SAMPLE:
            sample_pass()
        em.build()
    return nc


_CACHE = {}


def _consts(T):
    idn = np.eye(128, dtype=np.float32)
    p = np.arange(128)
    partner = np.where((p % 64) < 32, p + 32, p - 32)
    rot = np.zeros((128, 128), np.float32)
    rot[partner, p] = 1.0
    i = np.arange(128)[:, None]
    j = np.arange(128)[None, :]
    ms = (i > j).astype(np.float32)
    mi = (j >= i).astype(np.float32)
    e8 = np.zeros((8, 8, 128), np.float32)
    for h in range(8):
        e8[h, h, :] = 1.0
    e16 = np.zeros((16, 16, 128), np.float32)
    for b in range(16):
        e16[b, b, :] = 1.0
    half = 32
    inv_freq = (np.float32(10000.0) ** (-np.arange(half, dtype=np.float32) / np.float32(half))).astype(np.float32)
    fi = (p % 64) % 32
    sign = np.where((p % 64) < 32, -1.0, 1.0).astype(np.float32)

    def tables(pos):
        ang = pos.astype(np.float32)[None, :] * inv_freq[fi][:, None]
        ang = ang.astype(np.float32)
        return np.cos(ang).astype(np.float32), (np.sin(ang).astype(np.float32) * sign[:, None]).astype(np.float32)
    cp_, sp_ = tables(np.arange(T))
    cs_, ss_ = tables(np.array([16384]))
    return {
        "c_ident": idn, "c_rot": rot, "c_mstrict4": np.tile(ms, (1, 4)), "c_minclT4": np.tile(mi, (1, 4)),
        "c_ident4": np.tile(idn, (1, 4)), "c_e8": e8.reshape(8, 1024), "c_e16": e16.reshape(16, 2048),
        "c_cosp": cp_, "c_sinp": sp_, "c_coss": cs_, "c_sins": ss_,
    }


def run(inputs, NS, T, NB, n_cores, DO_SAMPLE=True, DO_PROMPT=True):
    f = lambda a: np.ascontiguousarray(np.asarray(a, dtype=np.float32))
    key = (NS, T, NB, DO_SAMPLE, DO_PROMPT)
    if key not in _CACHE:
        _CACHE[key] = build_program(NS, T, NB, DO_SAMPLE, DO_PROMPT)
    nc = _CACHE[key]
    shared = {}
    for nm in ("ffn1_w1", "ffn1_w3", "ffn1_w2", "w_in", "w_br_gdn", "w_br_swa", "w_out", "w_mem_q", "w_mem_k", "w_mem_v",
               "w_mem_o", "ffn2_w1", "ffn2_w3", "ffn2_w2"):
        shared[nm] = f(inputs[nm][0])
    norms = [inputs[k][0] for k in ("norm_ffn1", "norm_mix", "norm_mem_q", "norm_mem_kv", "norm_ffn2")] + [inputs["norm_final"]]
    g = np.stack([f(n).reshape(16, 128).T for n in norms], 1).reshape(128, 96)
    shared["gains"] = f(g)
    cw = f(inputs["conv_w"][0])
    shared["convw"] = f(cw.reshape(4, 24, 128).transpose(2, 1, 0).reshape(128, 96))
    shared["alog"] = f(inputs["gdn_A_log"][0]).reshape(8, 1)
    shared["dtb"] = f(inputs["gdn_dt_bias"][0]).reshape(8, 1)
    shared["gnorm"] = f(inputs["gdn_norm"][0]).reshape(128, 1)
    shared["sinks"] = f(inputs["swa_sinks"][0]).reshape(1, 16)
    shared["sinks_col"] = f(inputs["swa_sinks"][0]).reshape(16, 1)
    shared.update(_consts(T))
    in_maps = []
    for ci in range(n_cores):
        m = dict(shared)
        m["xp"] = f(inputs["x_prompt"][ci * NS:(ci + 1) * NS]).reshape(NS * T, D)
        m["xs"] = f(inputs["x_sample"][ci * NB:(ci + 1) * NB]).reshape(NB, D)
        m["sgdn"] = f(inputs["state_gdn"][0, ci * NB:(ci + 1) * NB]).reshape(NB * 1024, 128)
        m["sconv"] = f(inputs["state_conv"][0, ci * NB:(ci + 1) * NB]).reshape(NB * 3, 3072)
        m["ck"] = f(inputs["cache_swa_k"][0, ci * NB:(ci + 1) * NB]).reshape(NB * 128, 256)
        m["cv"] = f(inputs["cache_swa_v"][0, ci * NB:(ci + 1) * NB]).reshape(NB * 128, 256)
        m["cmk"] = f(inputs["cache_mem_k"][0, ci * NB:(ci + 1) * NB]).reshape(NB * 256, 512)
        m["cmv"] = f(inputs["cache_mem_v"][0, ci * NB:(ci + 1) * NB]).reshape(NB * 256, 512)
        m["memp"] = f(inputs["mem_prompt"][ci * NS:(ci + 1) * NS]).reshape(NS * 256, D)
        in_maps.append(m)
    res = run_bass_kernel_spmd(nc, in_maps, core_ids=list(range(n_cores)))
    R = res.results
    cat = lambda k: np.concatenate([np.asarray(r[k]) for r in R], 0)
    B = NS * n_cores
    BS = NB * n_cores
    y_p = cat("y_p").reshape(B, T, D)
    y_s = cat("y_s").reshape(BS, 1, D)
    outs = (y_p, y_s,
            cat("o_gdn_p").reshape(1, B, 8, 128, 128), cat("o_conv_p").reshape(1, B, 3, 3072),
            cat("o_k_p").reshape(1, B, 128, 4, 64), cat("o_v_p").reshape(1, B, 128, 4, 64),
            cat("o_mk_p").reshape(1, B, 256, 4, 128), cat("o_mv_p").reshape(1, B, 256, 4, 128),
            cat("o_gdn_s").reshape(1, BS, 8, 128, 128), cat("o_conv_s").reshape(1, BS, 3, 3072),
            cat("o_k_s").reshape(1, BS, 128, 4, 64), cat("o_v_s").reshape(1, BS, 128, 4, 64))
    return tuple(np.ascontiguousarray(o, dtype=np.float32) for o in outs)


def kernel(**inputs):
    return run(inputs, 2, 2048, 16, 8)
```

```python
import concourse.bass as bass
import concourse.mybir as mybir

F32 = mybir.dt.float32
BF16 = mybir.dt.bfloat16
I32 = mybir.dt.int32
ESZ = {F32: 4, BF16: 2, I32: 4, mybir.dt.float32r: 4}
ENGS = ("pe", "act", "dve", "pool", "sp")
SB_PAGE = 512
PS_PAGE = 2048


class _Op:
    __slots__ = ("eng", "fn", "deps", "flag", "rank", "dma", "label", "seq", "est", "pre", "succ", "prio", "npre", "t1")

    def __init__(self, eng, fn, deps, dma=None, label=None):
        self.seq = 0
        self.est = 0.3
        self.eng = eng
        self.fn = fn
        self.deps = deps
        self.flag = False
        self.rank = None
        self.dma = dma
        self.label = label


class Em:
    def __init__(self, nc):
        self.nc = nc
        self.ops = {e: [] for e in ENGS}
        self.pw = {}
        self.pr = {}
        self.dma_tot = {}
        self.tok_w = {}
        self.tok_r = {}
        self.final = []

    @staticmethod
    def pages(a):
        t = a.tensor
        sp = str(a.space) if hasattr(a, "space") else ""
        esz = ESZ[a.dtype]
        row = a.ap[0][0]
        off = a.offset % row if row > 0 else a.offset
        ext = 1
        for st, cnt in a.ap[1:]:
            ext += (cnt - 1) * abs(st)
        lo = off * esz
        hi = (off + ext) * esz
        pg = PS_PAGE if "PSUM" in sp.upper() or "psum" in t.name else SB_PAGE
        return [(t.name, p) for p in range(lo // pg, (hi - 1) // pg + 1)]

    def _collect(self, reads, writes, tr, tw):
        deps = []
        for a in reads:
            for p in self.pages(a):
                d = self.pw.get(p)
                if d is not None:
                    deps.append(d)
                if "psum" in p[0]:
                    deps.extend(self.pr.get(p, ()))
        for a in writes:
            for p in self.pages(a):
                d = self.pw.get(p)
                if d is not None:
                    deps.append(d)
                deps.extend(self.pr.get(p, ()))
        for t in tr:
            d = self.tok_w.get(t)
            if d is not None:
                deps.append(d)
        for t in tw:
            d = self.tok_w.get(t)
            if d is not None:
                deps.append(d)
            deps.extend(self.tok_r.get(t, ()))
        return deps

    def _commit(self, dep, reads, writes, tr, tw):
        for a in reads:
            for p in self.pages(a):
                self.pr.setdefault(p, []).append(dep)
        for a in writes:
            for p in self.pages(a):
                self.pw[p] = dep
                self.pr[p] = []
        for t in tr:
            self.tok_r.setdefault(t, []).append(dep)
        for t in tw:
            self.tok_w[t] = dep
            self.tok_r[t] = []

    def restamp(self, key, aps):
        dep = ("dma", key, self.dma_tot[key])
        for a in aps:
            for p in self.pages(a):
                self.pw[p] = dep

    def op(self, eng, fn, r=(), w=(), tr=(), tw=(), label=None, est=None):
        deps = self._collect(r, w, tr, tw)
        o = _Op(eng, fn, deps, label=label)
        self._seq = getattr(self, "_seq", 0) + 1
        o.seq = self._seq
        if est is None:
            n = 1
            if w:
                n = 1
                for st_, cnt in w[0].ap[1:]:
                    n *= cnt
            est = 0.12 + n / 960.0
        o.est = est
        self.ops[eng].append(o)
        self._commit(("op", o), r, w, tr, tw)
        return o

    def dma(self, q, key, out, in_, r=(), w=(), tr=(), tw=(), final=False, **kw):
        deps = self._collect(r, w, tr, tw)
        tot = self.dma_tot.get(key, 0) + 16
        self.dma_tot[key] = tot
        o = _Op(q, lambda e: e.dma_start(out=out, in_=in_, **kw), deps, dma=(key, tot))
        self._seq = getattr(self, "_seq", 0) + 1
        o.seq = self._seq
        o.est = 6.0
        self.ops[q].append(o)
        dep = ("dma", key, tot)
        self._commit(dep, r, w, tr, tw)
        if final:
            self.final.append(dep)
        return o

    def schedule(self, window=48):
        import heapq
        allops = [o for e in ENGS for o in self.ops[e]]
        dmap = {}
        for o in allops:
            if o.dma is not None:
                dmap[(o.dma[0], o.dma[1])] = o
        for o in allops:
            o.succ = []
        for o in allops:
            pre = {}
            for d in o.deps:
                p = d[1] if d[0] == "op" else dmap.get((d[1], d[2]))
                if p is not None and p is not o:
                    pre[id(p)] = p
            o.pre = list(pre.values())
            o.npre = len(o.pre)
            for p in o.pre:
                p.succ.append(o)
        for o in sorted(allops, key=lambda x: -x.seq):
            o.prio = o.est + max([s.prio for s in o.succ], default=0.0)
        fixed = ("pool", "sp")
        queues = {e: sorted(self.ops[e], key=lambda x: x.seq) for e in ENGS}
        ready = {e: [] for e in ENGS}
        ptr = {e: 0 for e in fixed}
        for e in ENGS:
            if e in fixed:
                continue
            for o in queues[e]:
                if o.npre == 0:
                    heapq.heappush(ready[e], (o.seq, id(o), o))
        free = {e: 0.0 for e in ENGS}
        order = {e: [] for e in ENGS}
        done = 0
        total = len(allops)
        INF = float("inf")
        while done < total:
            best = None
            for e in ENGS:
                if e in fixed:
                    if ptr[e] >= len(queues[e]):
                        continue
                    o = queues[e][ptr[e]]
                    if o.npre > 0:
                        continue
                    st = max([free[e]] + [p.t1 for p in o.pre])
                    cand = (st, -o.prio, o.seq, e, o)
                else:
                    if not ready[e]:
                        continue
                    cs = heapq.nsmallest(window, ready[e])
                    cand = None
                    for (_, _, o) in cs:
                        st = max([free[e]] + [p.t1 for p in o.pre])
                        k = (st, -o.prio, o.seq, e, o)
                        if cand is None or k[:3] < cand[:3]:
                            cand = k
                if best is None or cand[:3] < best[:3]:
                    best = cand
            assert best is not None, "scheduler deadlock"
            st, _, _, e, o = best
            if e in fixed:
                ptr[e] += 1
                issue = 1.3 if e == "pool" else 0.1
                free[e] = st + (issue if o.dma is not None else 0.0)
                o.t1 = st + (o.est if o.dma is not None else 0.0)
            else:
                ready[e].remove((o.seq, id(o), o))
                heapq.heapify(ready[e])
                free[e] = st + o.est
                o.t1 = st + o.est + 0.25
            order[e].append(o)
            done += 1
            for s in o.succ:
                s.npre -= 1
                if s.npre == 0 and s.eng not in fixed:
                    heapq.heappush(ready[s.eng], (s.seq, id(s), s))
        for e in ENGS:
            self.ops[e] = order[e]
        self.sim_time = max(free.values())

    def build(self, final_eng="sp"):
        nc = self.nc
        fin = _Op(final_eng, None, list(self.final))
        self._seq = getattr(self, "_seq", 0) + 1
        fin.seq = self._seq
        fin.est = 0.0
        self.ops[final_eng].append(fin)
        import os as _os2
        if int(_os2.environ.get("DBG_SCHED", "1")):
            self.schedule(int(_os2.environ.get("DBG_WINDOW", "48")))
        for e in ENGS:
            for o in self.ops[e]:
                for d in o.deps:
                    if d[0] == "op":
                        if d[1].eng == "pe" and o.eng == "pe":
                            continue
                        d[1].flag = True
        for e in ENGS:
            n = 0
            for o in self.ops[e]:
                if o.dma is None and o.flag:
                    n += 1
                    o.rank = n
        keys = sorted(self.dma_tot.keys())
        import contextlib
        with contextlib.ExitStack() as st:
            esem = {e: st.enter_context(nc.semaphore("s_" + e)) for e in ENGS}
            dsem = {k: st.enter_context(nc.semaphore("d_" + k)) for k in keys}
            block = st.enter_context(nc.Block())
            ops = self.ops

            def replay(ename, eng):
                waited = {}
                for o in ops[ename]:
                    need = {}
                    for d in o.deps:
                        if d[0] == "op":
                            po = d[1]
                            if po.eng == "pe" and ename == "pe":
                                continue
                            s = esem[po.eng]
                            v = po.rank
                            k = "e_" + po.eng
                        else:
                            s = dsem[d[1]]
                            v = d[2]
                            k = "d_" + d[1]
                        if need.get(k, (None, 0))[1] < v:
                            need[k] = (s, v)
                    for k, (s, v) in need.items():
                        if waited.get(k, 0) >= v:
                            continue
                        waited[k] = v
                        eng.wait_ge(s, v)
                    if o.fn is None:
                        continue
                    ins = o.fn(eng)
                    if o.dma is not None:
                        ins.then_inc(dsem[o.dma[0]], 16)
                    elif o.flag:
                        ins.then_inc(esem[ename], 1)

            @block.tensor
            def _(eng):
                replay("pe", eng)

            @block.scalar
            def _(eng):
                replay("act", eng)

            @block.vector
            def _(eng):
                replay("dve", eng)

            @block.gpsimd
            def _(eng):
                replay("pool", eng)

            @block.sync
            def _(eng):
                replay("sp", eng)

import numpy as np
import contextlib
from concourse.bass_utils import run_bass_kernel_spmd

AF = mybir.ActivationFunctionType
ALU = mybir.AluOpType

D = 2048
KC = 16
DFF = 5632
EPS = 1e-6
SW_Q0, SW_K0, SW_V0 = 4112, 5136, 5392
GG0, GS0 = 5648, 7696
NORM_FFN1, NORM_MIX, NORM_MEMQ, NORM_MEMKV, NORM_FFN2, NORM_FINAL = range(6)


import os as _os0
FP32R = int(_os0.environ.get("DBG_FP32R", "0"))


def _is_ap(x):
    return hasattr(x, "tensor") and hasattr(x, "ap")


class Ctx:
    def __init__(self, nc):
        self.nc = nc
        self.em = Em(nc)
        self._bank = 0
        self._bset = 0
        self._slot = 0
        self.reserved = set()
        self.scratch = None
        self.scr_index = {}
        self.scratch_n = 0

    def mm(self, out, lhsT, rhs, start=True, stop=True, fr=False):
        l2, r2 = lhsT, rhs
        if fr and FP32R:
            l2 = lhsT.bitcast(mybir.dt.float32r)
            r2 = rhs.bitcast(mybir.dt.float32r)
        n = 1
        for st_, cnt in rhs.ap[1:]:
            n *= cnt
        est = 0.03 + n / 1850.0 * (4.0 if rhs.dtype == F32 and not (fr and FP32R) else 1.0)
        self.em.op("pe", lambda e: e.matmul(out, l2, r2, start=start, stop=stop), r=[lhsT, rhs], w=[out], est=est)

    def tr(self, out, in_, ident):
        self.em.op("pe", lambda e: e.transpose(out, in_, ident), r=[in_, ident], w=[out], est=0.1)

    def act(self, out, in_, func, bias=None, scale=None, accum=None):
        kw = {}
        r = [in_]
        w = [out]
        if bias is not None:
            kw["bias"] = bias
            if _is_ap(bias):
                r.append(bias)
        if scale is not None:
            kw["scale"] = scale
            if _is_ap(scale):
                r.append(scale)
        if accum is not None:
            kw["accum_out"] = accum
            w.append(accum)
        self.em.op("act", lambda e: e.activation(out, in_, func, **kw), r=r, w=w)

    def tt(self, eng, out, a, b, op):
        self.em.op(eng, lambda e: e.tensor_tensor(out=out, in0=a, in1=b, op=op), r=[a, b], w=[out])

    def ts(self, eng, out, a, s1, op0, s2=None, op1=None):
        r = [a] + [s for s in (s1, s2) if _is_ap(s)]
        if op1 is None:
            self.em.op(eng, lambda e: e.tensor_scalar(out, a, s1, None, op0), r=r, w=[out])
        else:
            self.em.op(eng, lambda e: e.tensor_scalar(out, a, s1, s2, op0, op1), r=r, w=[out])

    def stt(self, out, a, scalar, b, op0, op1):
        r = [a, b] + ([scalar] if _is_ap(scalar) else [])
        self.em.op("dve", lambda e: e.scalar_tensor_tensor(out=out, in0=a, scalar=scalar, in1=b, op0=op0, op1=op1), r=r, w=[out])

    def cp(self, eng, out, in_):
        if eng == "act":
            self.em.op("act", lambda e: e.copy(out, in_), r=[in_], w=[out])
        else:
            self.em.op(eng, lambda e: e.tensor_copy(out=out, in_=in_), r=[in_], w=[out])

    def rsum(self, out, in_):
        self.em.op("dve", lambda e: e.reduce_sum(out=out, in_=in_, axis=mybir.AxisListType.X), r=[in_], w=[out])

    def memset(self, eng, ap, v):
        self.em.op(eng, lambda e: e.memset(ap, v), w=[ap])

    def bank(self):
        while True:
            b = self._bank
            self._bank = (b + 1) % 8
            if b not in self.reserved:
                return self.ps[:, b, :]

    def bankset(self):
        b = self._bset
        self._bset = 4 - b
        return b

    def slot(self):
        i = self._slot
        self._slot = (i + 1) % len(self.wslots)
        return self.wslots[i], "ws%d" % i

    def load_w(self, W, k0, kn, pieces):
        merged = []
        for p in pieces:
            if merged and merged[-1][0] + merged[-1][2] == p[0] and merged[-1][1] + merged[-1][2] == p[1]:
                merged[-1][2] += p[2]
            else:
                merged.append(list(p))
        out = []
        for b0 in range(0, kn, 8):
            bn = min(8, kn - b0)
            slot, key = self.slot()
            bkey = (W.tensor.name, k0 + b0, bn, tuple(tuple(m) for m in merged))
            use_scr = self.scratch is not None and bkey in self.scr_index
            if use_scr:
                idx = self.scr_index[bkey]
                self.em.dma("pool", key, slot[:, :, :], self.scratch[idx], w=[slot[:, :, :]], tr=["scr%d" % idx])
            else:
                aps = []
                for (o2, c0, w) in merged:
                    dst = slot[:, 0:bn, o2:o2 + w]
                    src = W[(k0 + b0) * 128:(k0 + b0 + bn) * 128, c0:c0 + w].rearrange("(kc p) f -> p kc f", p=128)
                    self.em.dma("pool", key, dst, src, w=[dst])
                    aps.append(dst)
                self.em.restamp(key, aps)
                if self.scratch is not None and len(self.scr_index) < self.scratch_n:
                    idx = len(self.scr_index)
                    self.scr_index[bkey] = idx
                    self.em.dma("sp", "scrst%d" % (idx % 6), self.scratch[idx], slot[:, :, :], r=[slot[:, :, :]], tw=["scr%d" % idx])
            for j in range(bn):
                out.append(slot[:, j, :])
        return out

    def dense(self, W, nk, subs, rhs, N, evac):
        groups = [subs[i:i + 4] for i in range(0, len(subs), 4)]
        for gi, grp in enumerate(groups):
            base = self.bankset()
            self.reserved = set(range(base, base + len(grp)))
            pieces = []
            for si, sub in enumerate(grp):
                o2 = si * 128
                for (c0, w) in sub:
                    pieces.append((o2, c0, w))
                    o2 += w
            for k0 in range(0, nk, 8):
                kn = min(8, nk - k0)
                wt = self.load_w(W, k0, kn, pieces)
                for si, sub in enumerate(grp):
                    wd = sum(w for _, w in sub)
                    for kk in range(kn):
                        self.mm(self.ps[0:wd, base + si, 0:N], wt[kk][:, si * 128:si * 128 + wd], rhs(k0 + kk),
                                start=(k0 + kk == 0), stop=(k0 + kk == nk - 1))
                    if k0 + kn == nk:
                        evac(gi * 4 + si, self.ps[0:wd, base + si, 0:N])
            self.reserved = set()


def nat(c0, n):
    return [[(c0 + i * 128, 128)] for i in range(n)]


def build_program(NS, T, NB, DO_SAMPLE=True, DO_PROMPT=True):
    nc = bass.Bass("TRN2", target_bir_lowering=False)
    c = Ctx(nc)
    em = c.em
    NTL = T // 512

    def din(name, shape):
        return nc.dram_tensor(name, list(shape), F32, kind="ExternalInput").ap()

    def dout(name, shape):
        return nc.dram_tensor(name, list(shape), F32, kind="ExternalOutput").ap()

    XP = din("xp", [NS * T, D]); XS = din("xs", [NB, D])
    SG = din("sgdn", [NB * 8 * 128, 128]); SC = din("sconv", [NB * 3, 3072])
    CK = din("ck", [NB * 128, 256]); CV = din("cv", [NB * 128, 256])
    CMK = din("cmk", [NB * 256, 512]); CMV = din("cmv", [NB * 256, 512])
    MEMP = din("memp", [NS * 256, D])
    W = {}
    for nm, shp in [("ffn1_w1", [D, DFF]), ("ffn1_w3", [D, DFF]), ("ffn1_w2", [DFF, D]), ("w_in", [D, 9744]),
                    ("w_br_gdn", [1024, D]), ("w_br_swa", [1024, D]), ("w_out", [D, D]), ("w_mem_q", [D, 512]),
                    ("w_mem_k", [D, 512]), ("w_mem_v", [D, 512]), ("w_mem_o", [512, D]),
                    ("ffn2_w1", [D, DFF]), ("ffn2_w3", [D, DFF]), ("ffn2_w2", [DFF, D])]:
        W[nm] = din(nm, shp)
    GAINS = din("gains", [128, 96]); CONVW = din("convw", [128, 96])
    ALOG = din("alog", [8, 1]); DTB = din("dtb", [8, 1]); GN = din("gnorm", [128, 1]); SINKS = din("sinks", [1, 16])
    C_ID = din("c_ident", [128, 128]); C_ROT = din("c_rot", [128, 128])
    C_MS = din("c_mstrict4", [128, 512]); C_MI = din("c_minclT4", [128, 512]); C_ID4 = din("c_ident4", [128, 512])
    C_E8 = din("c_e8", [8, 1024]); C_E16 = din("c_e16", [16, 2048])
    C_COSP = din("c_cosp", [128, T]); C_SINP = din("c_sinp", [128, T])
    C_COSS = din("c_coss", [128, 1]); C_SINS = din("c_sins", [128, 1]); SINKC = din("sinks_col", [16, 1])

    YP = dout("y_p", [NS * T, D]); YS = dout("y_s", [NB, D])
    OGP = dout("o_gdn_p", [NS * 8 * 128, 128]); OCP = dout("o_conv_p", [NS * 3, 3072])
    OKP = dout("o_k_p", [NS * 128, 256]); OVP = dout("o_v_p", [NS * 128, 256])
    OMKP = dout("o_mk_p", [NS * 256, 512]); OMVP = dout("o_mv_p", [NS * 256, 512])
    OGS = dout("o_gdn_s", [NB * 8 * 128, 128]); OCS = dout("o_conv_s", [NB * 3, 3072])
    OKS = dout("o_k_s", [NB * 128, 256]); OVS = dout("o_v_s", [NB * 128, 256])

    SCR_H = 112
    SCR_N = 2 * SCR_H
    scr_ts = [nc.dram_tensor("wscratch%d" % j, [SCR_H, 128, 4096], BF16, kind="Internal").ap() for j in range(2)]
    c.scratch = [scr_ts[i // SCR_H][i % SCR_H].rearrange("p (k f) -> p k f", f=512) for i in range(SCR_N)]
    c.scratch_n = SCR_N
    st = contextlib.ExitStack()
    with st:
        def sb(name, shape, dt=F32):
            return st.enter_context(nc.sbuf_tensor(name, list(shape), dt))

        c.ps = st.enter_context(nc.psum_tensor("psum", [128, 8, 512], F32))
        ps = c.ps
        xres = sb("xres", [128, KC, 512]); xn = sb("xn", [128, KC, 512], BF16)
        c.wslots = [sb("ws%d" % i, [128, 8, 512], BF16) for i in range(6)]
        big = sb("big", [128, 8192])
        obr = sb("obr", [128, 8, 512], BF16); merged = sb("merged", [128, KC, 512], BF16)
        ident = sb("ident", [128, 128]); rot = sb("rot", [128, 128]); ones_f = sb("ones_f", [128, 128])
        ones_b = sb("ones_b", [128, 128], BF16)
        ms4b = sb("ms4b", [128, 512], BF16); mi4b = sb("mi4b", [128, 512], BF16); id4 = sb("id4b", [128, 512], BF16)
        ms4 = ms4b; mi4 = mi4b
        gains = sb("gains_s", [128, 96]); convw = sb("convw_s", [128, 96])
        alog = sb("alog_s", [8, 1]); dtb = sb("dtb_s", [8, 1]); gn = sb("gn_s", [128, 1]); sk = sb("sk_s", [1, 16])
        nega = sb("nega", [8, 1]); sinkcol = sb("sinkcol", [128, 16])
        rstd = sb("rstd", [128, 512]); sqt = sb("sqt", [128, 2, 512], BF16)
        ropes = sb("ropes", [128, 2])
        halo = sb("halo", [128, 24, 3]); Sst = sb("Sst", [128, 8, 128])
        kLo = sb("kLo", [128, 4, 640], BF16); kHi = sb("kHi", [128, 4, 640], BF16); vdup = sb("vdup", [128, 5, 4, 128], BF16)
        mkT = sb("mkT", [128, 4, 256], BF16); mvtok = sb("mvtok", [128, 2, 512], BF16)
        g8 = sb("g8", [8, 3, 512]); dg = sb("dg", [8, 256])
        colb = sb("colb", [128, 6, 32])
        ssq = sb("ssq", [128, 8]); junk = sb("junk", [128, 512]); tmpb = sqt
        mflat = merged[:, :, :].rearrange("p a n -> p (a n)").bitcast(F32)
        chn = mflat[:, 0:3072].rearrange("p (a n) -> p a n", n=512)
        e8 = mflat[0:8, 3072:4096]

        def bigv(off, n, dt=F32):
            a = big[:, off:off + n]
            return a.bitcast(dt) if dt != F32 else a

        for dst, src in [(ident, C_ID), (rot, C_ROT),
                         (gains, GAINS), (convw, CONVW), (alog, ALOG), (dtb, DTB), (gn, GN), (sk, SINKS),
                         (ropes[:, 0:1], C_COSS), (ropes[:, 1:2], C_SINS)]:
            d = dst[:] if not _is_ap(dst) else dst
            em.dma("sp", "c_" + d.tensor.name + str(d.offset), d, src, w=[d])
        c.memset("dve", ones_f[:], 1.0)
        c.memset("pool", kLo[:, :, :], 0.0)
        c.memset("pool", kHi[:, :, :], 0.0)
        c.cp("dve", ones_b[:], ones_f[:])
        for dstb, srcc in ((ms4b, C_MS), (mi4b, C_MI), (id4, C_ID4)):
            em.dma("sp", "cst_tmp", big[:, 0:512], srcc, w=[big[:, 0:512]])
            c.cp("dve", dstb[:], big[:, 0:512])
        c.act(nega[:], alog[:], AF.Exp)
        c.ts("dve", nega[:], nega[:], -1.0, ALU.mult)
        c.act(sk[:], sk[:], AF.Exp)
        b0 = c.bank()
        c.mm(b0[:, 0:16], ones_f[0:1, :], sk[0:1, :])
        c.cp("dve", sinkcol[:, 0:16], b0[:, 0:16])

        def norm(gi, N, out_fn=None):
            b = c.bank()
            for kc in range(KC):
                sq = sqt[:, kc % 2, 0:N]
                if kc % 2 == 0 or not int(_os0.environ.get("DBG_V4", "0")):
                    c.act(sq, xres[:, kc, 0:N], AF.Square)
                else:
                    c.tt("dve", sq, xres[:, kc, 0:N], xres[:, kc, 0:N], ALU.mult)
                c.mm(b[:, 0:N], ones_b[:], sq, start=(kc == 0), stop=(kc == KC - 1))
            rsq(rstd[:, 0:N], b[:, 0:N], 1.0 / D)
            for kc in range(KC):
                o = xn[:, kc, 0:N] if out_fn is None else out_fn(kc)
                c.stt(o, xres[:, kc, 0:N], gains[:, gi * 16 + kc:gi * 16 + kc + 1], rstd[:, 0:N], ALU.mult, ALU.mult)

        epsb = sb("epsb", [128, 4])
        c.memset("dve", epsb[:, 0:1], EPS)
        c.memset("dve", epsb[:, 1:2], float(np.log(128.0 ** -0.5)))
        c.memset("dve", epsb[:, 2:3], 1.0)
        c.memset("dve", epsb[:, 3:4], 0.0)

        def rsq(out, in_, scale, biascol=0, post_bias=None):
            c.act(out, in_, AF.Ln, scale=scale, bias=epsb[0:out.ap[0][1], biascol:biascol + 1])
            if post_bias is None:
                c.act(out, out, AF.Exp, scale=-0.5)
            else:
                c.act(out, out, AF.Exp, scale=-0.5, bias=post_bias)

        silu4 = sb("silu4", [128, 4, 512], BF16)

        def ffn(w1n, w3n, w2n, gi, N):
            norm(gi, N)
            silu_t = [silu4[:, m, 0:N] for m in range(4)]
            hb = bigv(0, 22 * 256, BF16).rearrange("p (c n) -> p c n", n=512)
            for half in range(2):
                f0 = half * 22
                for g0 in range(0, 22, 4):
                    ng = min(4, 22 - g0)
                    pc = [(0, (f0 + g0) * 128, ng * 128)]
                    for k0 in (0, 8):
                        w1 = c.load_w(W[w1n], k0, 8, pc)
                        w3 = c.load_w(W[w3n], k0, 8, pc)
                        for m in range(ng):
                            for kk in range(8):
                                c.mm(ps[:, m, 0:N], w1[kk][:, m * 128:(m + 1) * 128], xn[:, k0 + kk, 0:N], start=(k0 + kk == 0), stop=(k0 + kk == 15))
                            if k0 == 8:
                                c.act(silu_t[m], ps[:, m, 0:N], AF.Silu)
                        for m in range(ng):
                            for kk in range(8):
                                c.mm(ps[:, 4 + m, 0:N], w3[kk][:, m * 128:(m + 1) * 128], xn[:, k0 + kk, 0:N], start=(k0 + kk == 0), stop=(k0 + kk == 15))
                            if k0 == 8:
                                c.tt("dve", hb[:, g0 + m, 0:N], silu_t[m], ps[:, 4 + m, 0:N], ALU.mult)
                for og in range(4):
                    base = c.bankset()
                    for (k0, kn) in [(0, 8), (8, 8), (16, 6)]:
                        w2 = c.load_w(W[w2n], f0 + k0, kn, [(0, og * 512, 512)])
                        for m in range(4):
                            for kk in range(kn):
                                c.mm(ps[:, base + m, 0:N], w2[kk][:, m * 128:(m + 1) * 128], hb[:, k0 + kk, 0:N],
                                     start=(k0 + kk == 0), stop=(k0 + kk == 21))
                    for m in range(4):
                        xr = xres[:, og * 4 + m, 0:N]
                        c.stt(xr, ps[:, base + m, 0:N], 0.5, xr, ALU.mult, ALU.add)

        def branch(wbr, gate0, first, N):
            for g in range(4):
                wA = c.load_w(W[wbr], 0, 8, [(0, g * 512, 512)])
                wB = c.load_w(W["w_in"], 0, 16, [(0, gate0 + g * 512, 512)])
                for m in range(4):
                    for kk in range(8):
                        c.mm(ps[:, m, 0:N], wA[kk][:, m * 128:(m + 1) * 128], obr[:, kk, 0:N], start=(kk == 0), stop=(kk == 7))
                    for kk in range(16):
                        c.mm(ps[:, 4 + m, 0:N], wB[kk][:, m * 128:(m + 1) * 128], xn[:, kk, 0:N], start=(kk == 0), stop=(kk == 15))
                    t = tmpb[:, m % 2, 0:N]
                    c.act(t, ps[:, 4 + m, 0:N], AF.Sigmoid)
                    mg = merged[:, g * 4 + m, 0:N]
                    if first:
                        c.tt("dve", mg, t, ps[:, m, 0:N], ALU.mult)
                    else:
                        c.tt("dve", t, t, ps[:, m, 0:N], ALU.mult)
                        c.tt("dve", mg, mg, t, ALU.add)

        def add_res(N):
            def ev(i, p):
                xr = xres[:, i, 0:N]
                c.tt("dve", xr, xr, p, ALU.add)
            return ev

        def l2n(out, y, N, post_bias, slot=0):
            sq = sqt[:, slot, 0:N]
            c.act(sq, y, AF.Square)
            b = c.bank()
            c.mm(b[:, 0:N], ones_b[:], sq)
            r = (junk if slot == 0 else rstd)[:, 0:N]
            rsq(r, b[:, 0:N], 1.0, 0, post_bias)
            c.tt("dve", out, y, r, ALU.mult)

        def final_out(N, Ydst, tokmajor_rows):
            nb = max(1, N // 128)
            pw = min(N, 128)
            ystage = bigv(0, nb * 2048).rearrange("p (b d) -> p b d", d=2048)
            yT = junk
            b = c.bank()
            for kc in range(KC):
                sq = sqt[:, kc % 2, 0:N]
                if kc % 2 == 0 or not int(_os0.environ.get("DBG_V4", "0")):
                    c.act(sq, xres[:, kc, 0:N], AF.Square)
                else:
                    c.tt("dve", sq, xres[:, kc, 0:N], xres[:, kc, 0:N], ALU.mult)
                c.mm(b[:, 0:N], ones_b[:], sq, start=(kc == 0), stop=(kc == KC - 1))
            rsq(rstd[:, 0:N], b[:, 0:N], 1.0 / D)
            for kc in range(KC):
                c.stt(yT[:, 0:N], xres[:, kc, 0:N], gains[:, NORM_FINAL * 16 + kc:NORM_FINAL * 16 + kc + 1], rstd[:, 0:N], ALU.mult, ALU.mult)
                b2 = c.bank()
                for bl in range(nb):
                    c.tr(b2[0:pw, bl * 128:(bl + 1) * 128], yT[:, bl * 128:bl * 128 + pw], ident[:])
                c.cp("act", ystage[0:pw, :, kc * 128:(kc + 1) * 128], b2[0:pw, 0:nb * 128].rearrange("p (b d) -> p b d", d=128))
            if N >= 128:
                em.dma("sp", "ystage", Ydst.rearrange("(b p) d -> p b d", p=128), ystage[:, :, :], r=[ystage[:, :, :]], final=True)
            else:
                em.dma("sp", "ystage", Ydst, ystage[0:pw, 0, :], r=[ystage[0:pw, 0, :]], final=True)

        def load_x(Xsrc, N):
            nb = max(1, N // 128)
            pw = min(N, 128)
            xin = bigv(0, nb * 2048).rearrange("p (b d) -> p b d", d=2048)
            if N >= 128:
                em.dma("sp", "xin", xin[:, :, :], Xsrc.rearrange("(b p) d -> p b d", p=128), w=[xin[:, :, :]])
            else:
                em.dma("sp", "xin", xin[0:pw, 0, :], Xsrc, w=[xin[0:pw, 0, :]])
            for kc in range(KC):
                b = c.bank()
                for bl in range(nb):
                    c.tr(b[:, bl * 128:bl * 128 + pw], xin[0:pw, bl, kc * 128:(kc + 1) * 128], ident[0:pw, 0:pw])
                c.cp("act" if kc % 2 else "dve", xres[:, kc, 0:N], b[:, 0:N])

        xflat = xn[:, :, :].rearrange("p a n -> p (a n)").bitcast(F32)
        xinA = xflat.rearrange("p (b d) -> p b d", d=2048)
        xinB = mflat.rearrange("p (b d) -> p b d", d=2048)

        def issue_x(row0):
            em.dma("sp", "xinA", xinA, XP[row0:row0 + 256, :].rearrange("(b p) d -> p b d", p=128), w=[xinA])
            em.dma("sp", "xinB", xinB, XP[row0 + 256:row0 + 512, :].rearrange("(b p) d -> p b d", p=128), w=[xinB])

        def consume_x():
            for kc in range(KC):
                b = c.bank()
                for bl in range(4):
                    src = (xinA if bl < 2 else xinB)[:, bl % 2, kc * 128:(kc + 1) * 128]
                    c.tr(b[:, bl * 128:(bl + 1) * 128], src, ident[:])
                c.cp("act" if kc % 2 else "dve", xres[:, kc, 0:512], b[:, 0:512])

        def mem_kv(s):
            mst = bigv(0, 4096).rearrange("p (b d) -> p b d", d=2048)
            jb = bigv(4096, 2048)
            mstage = bigv(6144, 1024).rearrange("p (b d) -> p b d", d=512)
            em.dma("sp", "mst", mst[:, :, :], MEMP[s * 256:(s + 1) * 256, :].rearrange("(b p) d -> p b d", p=128), w=[mst[:, :, :]])
            for bl in range(2):
                c.act(jb, mst[:, bl, :], AF.Square)
                c.rsum(ssq[:, bl:bl + 1], jb)
            rsq(ssq[:, 2:4], ssq[:, 0:2], 1.0 / D)
            for bl in range(2):
                c.ts("dve", mst[:, bl, :], mst[:, bl, :], ssq[:, 2 + bl:3 + bl], ALU.mult)
            mh = xn
            _mk = int(_os.environ.get("DBG_MK", "9"))
            if _mk < 2:
                return
            for kc in range(KC):
                b = c.bank()
                for bl in range(2):
                    c.tr(b[:, bl * 128:(bl + 1) * 128], mst[:, bl, kc * 128:(kc + 1) * 128], ident[:])
                c.ts("dve", mh[:, kc, 0:256], b[:, 0:256], gains[:, NORM_MEMKV * 16 + kc:NORM_MEMKV * 16 + kc + 1], ALU.mult)
            if _mk < 3:
                return
            for wn, OUT, isv in (("w_mem_k", OMKP, False), ("w_mem_v", OMVP, True)):
                wt = c.load_w(W[wn], 0, 16, [(0, 0, 512)])
                if _mk < 4:
                    continue
                if not isv:
                    for m in range(4):
                        b = c.bank()
                        for kk in range(16):
                            c.mm(b[:, 0:256], wt[kk][:, m * 128:(m + 1) * 128], mh[:, kk, 0:256], start=(kk == 0), stop=(kk == 15))
                        c.cp("act", mkT[:, m, :], b[:, 0:256])
                if _mk < 5:
                    continue
                for bl in range(2):
                    b = c.bank()
                    for kk in range(16):
                        c.mm(b[:, :], mh[:, kk, bl * 128:(bl + 1) * 128], wt[kk], start=(kk == 0), stop=(kk == 15))
                    c.cp("act", mstage[:, bl, :], b[:, :])
                    if isv:
                        c.cp("dve", mvtok[:, bl, :], b[:, :])
                if _mk >= 6:
                    for bl in range(2):
                        em.dma("sp", "mstage%d" % bl, OUT[s * 256 + bl * 128:s * 256 + (bl + 1) * 128, :], mstage[:, bl, :],
                               r=[mstage[:, bl, :]], final=True)

        def conv_p(ch, p, cb, y, N):
            c.cp("act", cb[:, 0:3], halo[:, ch, :])
            c.cp("act", cb[:, 3:3 + N], p)
            c.cp("act", halo[:, ch, :], cb[:, N:N + 3])
            c.ts("dve", y, cb[:, 3:3 + N], convw[:, ch * 4 + 3:ch * 4 + 4], ALU.mult)
            for i in (2, 1, 0):
                c.stt(y, cb[:, i:i + N], convw[:, ch * 4 + i:ch * 4 + i + 1], y, ALU.mult, ALU.add)
            c.act(y, y, AF.Silu)

        def gdn_prompt(N):
            NBk = N // 128
            em.dma("sp", "e8", e8, C_E8[:, :], w=[e8])
            gcol, bcol, gamc, ngam, kdec, gend = [colb[:, i, :] for i in range(6)]
            bT, gT, GT = g8[:, 0, :], g8[:, 1, :], g8[:, 2, :]

            def ev_ba(i, p):
                if i == 0:
                    c.act(bT[:, 0:N], p, AF.Sigmoid)
                else:
                    c.act(gT[:, 0:N], p, AF.Exp, bias=dtb[:])
                    c.act(gT[:, 0:N], gT[:, 0:N], AF.Ln, bias=epsb[0:8, 2:3])
                    c.ts("dve", gT[:, 0:N], gT[:, 0:N], nega[:], ALU.mult)
            c.dense(W["w_in"], 16, [[(4096, 8)], [(4104, 8)]], lambda kc: xn[:, kc, 0:N], N, ev_ba)
            for bl in range(NBk):
                em.op("dve", lambda e, bl=bl: e.tensor_tensor_scan(out=GT[:, bl * 128:(bl + 1) * 128], data0=ones_f[0:8, 0:128],
                                                                    data1=gT[:, bl * 128:(bl + 1) * 128], initial=0.0,
                                                                    op0=ALU.mult, op1=ALU.add),
                      r=[gT[:, bl * 128:(bl + 1) * 128]], w=[GT[:, bl * 128:(bl + 1) * 128]])
            b = c.bank()
            for bl in range(NBk):
                c.tr(b[:, bl * 8:bl * 8 + 8], GT[:, bl * 128:(bl + 1) * 128], ident[0:8, 0:8])
                c.tr(b[:, 32 + bl * 8:32 + bl * 8 + 8], bT[:, bl * 128:(bl + 1) * 128], ident[0:8, 0:8])
            c.cp("dve", gcol[:, 0:NBk * 8], b[:, 0:NBk * 8])
            c.cp("dve", bcol[:, 0:NBk * 8], b[:, 32:32 + NBk * 8])
            c.act(gamc, gcol, AF.Exp)
            c.ts("dve", ngam, gamc, -1.0, ALU.mult)
            for bl in range(NBk):
                c.ts("dve", dg[:, bl * 8:bl * 8 + 8], ident[0:8, 0:8], GT[:, bl * 128 + 127:bl * 128 + 128], ALU.mult)
            b = c.bank()
            c.mm(b[:, 0:NBk * 8], ones_f[0:8, :], dg[:, 0:NBk * 8])
            c.cp("dve", gend, b[:, 0:32])
            c.tt("dve", kdec, gend, gcol, ALU.subtract)
            c.act(kdec, kdec, AF.Exp)
            c.act(gend, gend, AF.Exp)

            Grow = bigv(0, 512); gamrow = bigv(512, 512)
            cbq = bigv(1024, 520); cbk = bigv(1544, 520); cbv = bigv(2064, 520)
            yq = bigv(2584, 512); E = bigv(3096, 512); Dm = bigv(3608, 512)
            qT = bigv(4120, 256, BF16); qgT = bigv(4376, 256, BF16); kT = bigv(4632, 256, BF16)
            Kend = bigv(4888, 256, BF16); Vtok = bigv(5144, 256, BF16); MT = bigv(5400, 256, BF16)
            QKDT = bigv(5656, 256, BF16); szb = bigv(5912, 256, BF16)
            X = bigv(6168, 64, BF16); U = bigv(6232, 64, BF16); On = bigv(6296, 64, BF16); Sbf = bigv(6360, 64, BF16)
            P = [chn[:, 0, :], chn[:, 1, :]]; PT = [chn[:, 2, :], chn[:, 3, :]]; TT = [chn[:, 4, :], chn[:, 5, :]]
            idb = id4[:, 0:128]
            Af = mflat[:, 0:512]
            Pb = [mflat[:, 512:768].bitcast(BF16), mflat[:, 768:1024].bitcast(BF16)]
            PTb = [mflat[:, 1024:1280].bitcast(BF16), mflat[:, 1280:1536].bitcast(BF16)]
            TTb = [mflat[:, 1536:1792].bitcast(BF16), mflat[:, 1792:2048].bitcast(BF16)]
            Xf = mflat[:, 2048:2560]; Wf = mflat[:, 512:1024]; Tn = mflat[:, 1024:1536]

            def blk(a, bl):
                return a[:, bl * 128:(bl + 1) * 128]

            def head_dense(h):
                base = c.bankset()
                c.reserved = set(range(base, base + 4))
                cols = [h * 128, 1024 + h * 128, 2048 + h * 128, 3072 + h * 128]
                pieces = [(si * 128, cols[si], 128) for si in range(4)]
                for k0 in (0, 8):
                    wt = c.load_w(W["w_in"], k0, 8, pieces)
                    for si in range(4):
                        for kk in range(8):
                            c.mm(ps[:, base + si, 0:N], wt[kk][:, si * 128:(si + 1) * 128], xn[:, k0 + kk, 0:N],
                                 start=(k0 + kk == 0), stop=(k0 + kk == 15))
                return base

            def head_copies(h, base):
                for i, cbx in enumerate((cbq, cbk, cbv)):
                    ch = i * 8 + h
                    c.cp("act", cbx[:, 0:3], halo[:, ch, :])
                    c.cp("act" if i != 1 else "dve", cbx[:, 3:3 + N], ps[:, base + i, 0:N])
                    c.cp("act", halo[:, ch, :], cbx[:, N:N + 3])
                c.act(szb[:, 0:N], ps[:, base + 3, 0:N], AF.Silu)
                c.reserved = set()

            def conv_m(ch, cbx, y):
                if int(_os.environ.get("DBG_W1", "1")):
                    c.act(y, cbx[:, 3:3 + N], AF.Copy, scale=convw[:, ch * 4 + 3:ch * 4 + 4])
                else:
                    c.ts("dve", y, cbx[:, 3:3 + N], convw[:, ch * 4 + 3:ch * 4 + 4], ALU.mult)
                for i in (2, 1, 0):
                    c.stt(y, cbx[:, i:i + N], convw[:, ch * 4 + i:ch * 4 + i + 1], y, ALU.mult, ALU.add)
                c.act(y, y, AF.Silu)

            V1 = int(_os.environ.get("DBG_V1", "1"))
            V2 = int(_os.environ.get("DBG_V2", "1"))
            yqs = [yq, bigv(6424, 512), bigv(6936, 512)] if V1 else [yq, yq, yq]
            X_ = [X, bigv(7448, 64, BF16)] if V2 else [X, X]
            U_ = [U, bigv(7512, 64, BF16)] if V2 else [U, U]
            On_ = [On, bigv(7576, 64, BF16)] if V2 else [On, On]

            def head_process(h):
                conv_m(h, cbq, yqs[0][:, 0:N])
                l2n(qT[:, 0:N], yqs[0][:, 0:N], N, epsb[:, 1:2], 0)
                conv_m(8 + h, cbk, yqs[1][:, 0:N])
                l2n(kT[:, 0:N], yqs[1][:, 0:N], N, None, 1 if V1 else 0)
                conv_m(16 + h, cbv, yqs[2][:, 0:N])
                b = c.bank()
                for bl in range(NBk):
                    c.tr(blk(b, bl), blk(yqs[2], bl), ident[:])
                c.cp("act", Vtok[:, 0:N], b[:, 0:N])

            hbase = head_dense(0)
            for h in range(8):
                head_copies(h, hbase)
                if h < 7:
                    hbase = head_dense(h + 1)
                head_process(h)
                Sh = Sst[:, h, :]
                c.cp("act", Sbf, Sh)
                b = c.bank()
                c.mm(b[:, 0:N], e8[:, h * 128:(h + 1) * 128], GT[:, 0:N])
                c.cp("act", Grow[:, 0:N], b[:, 0:N])
                c.act(gamrow[:, 0:N], Grow[:, 0:N], AF.Exp)
                c.tt("dve", qgT[:, 0:N], qT[:, 0:N], gamrow[:, 0:N], ALU.mult)
                bb = c.bank().bitcast(BF16)
                for bl in range(NBk):
                    c.tr(blk(bb, bl), blk(kT, bl), idb)
                for bl in range(NBk):
                    if int(_os.environ.get("DBG_W2", "0")):
                        c.act(blk(Kend, bl), blk(bb, bl), AF.Copy, scale=kdec[:, bl * 8 + h:bl * 8 + h + 1])
                    else:
                        c.ts("dve", blk(Kend, bl), blk(bb, bl), kdec[:, bl * 8 + h:bl * 8 + h + 1], ALU.mult)
                bKK = c.bank()
                for bl in range(NBk):
                    c.mm(blk(bKK, bl), blk(kT, bl), blk(kT, bl))
                    c.ts("dve", blk(E, bl), blk(Grow, bl), gcol[:, bl * 8 + h:bl * 8 + h + 1], ALU.subtract, 0.0, ALU.max)
                c.act(E[:, 0:N], E[:, 0:N], AF.Exp, scale=-1.0)
                for bl in range(NBk):
                    c.stt(blk(Dm, bl), blk(E, bl), bcol[:, bl * 8 + h:bl * 8 + h + 1], blk(ms4, bl), ALU.mult, ALU.mult)
                c.tt("dve", Af[:, 0:N], bKK[:, 0:N], Dm[:, 0:N], ALU.mult)
                c.cp("act", Pb[0][:, 0:N], Af[:, 0:N])
                bb = c.bank().bitcast(BF16)
                for bl in range(NBk):
                    c.tr(blk(bb, bl), blk(Pb[0], bl), idb)
                c.cp("act", PTb[0][:, 0:N], bb[:, 0:N])
                c.tt("dve", TTb[0][:, 0:N], id4[:, 0:N], PTb[0][:, 0:N], ALU.subtract)
                a, ta = 0, 0
                for lvl in range(1, 7):
                    b1 = c.bank()
                    for bl in range(NBk):
                        c.mm(blk(b1, bl), blk(PTb[a], bl), blk(Pb[a], bl))
                    c.cp("act", Pb[1 - a][:, 0:N], b1[:, 0:N])
                    if lvl < 6:
                        b2 = c.bank()
                        for bl in range(NBk):
                            c.mm(blk(b2, bl), blk(Pb[a], bl), blk(PTb[a], bl))
                        c.cp("dve", PTb[1 - a][:, 0:N], b2[:, 0:N])
                    b3 = c.bank()
                    for bl in range(NBk):
                        c.mm(blk(b3, bl), blk(Pb[1 - a], bl), blk(TTb[ta], bl))
                    c.tt("dve", TTb[1 - ta][:, 0:N], b3[:, 0:N], TTb[ta][:, 0:N], ALU.add)
                    a = 1 - a
                    ta = 1 - ta
                Xb = TTb[ta]
                c.cp("act", Xf[:, 0:N], Xb[:, 0:N])
                bb = c.bank().bitcast(BF16)
                for bl in range(NBk):
                    c.tr(blk(bb, bl), blk(Xb, bl), idb)
                c.cp("dve", Tn[:, 0:N], bb[:, 0:N])
                bR = c.bank()
                for bl in range(NBk):
                    c.mm(blk(bR, bl), blk(Af, bl), blk(Xf, bl))
                c.stt(Wf[:, 0:N], bR[:, 0:N], -1.0, Xf[:, 0:N], ALU.mult, ALU.subtract)
                c.stt(Wf[:, 0:N], id4[:, 0:N], 2.0, Wf[:, 0:N], ALU.mult, ALU.add)
                bX = c.bank()
                for bl in range(NBk):
                    c.mm(blk(bX, bl), blk(Tn, bl), blk(Wf, bl))
                for bl in range(NBk):
                    c.ts("dve", blk(MT, bl), blk(bX, bl), bcol[:, bl * 8 + h:bl * 8 + h + 1], ALU.mult)
                bQ = c.bank()
                for bl in range(NBk):
                    c.mm(blk(bQ, bl), blk(kT, bl), blk(qT, bl))
                    c.ts("dve", blk(E, bl), blk(Grow, bl), gcol[:, bl * 8 + h:bl * 8 + h + 1], ALU.subtract, 0.0, ALU.min)
                c.act(E[:, 0:N], E[:, 0:N], AF.Exp)
                c.tt("dve", E[:, 0:N], E[:, 0:N], mi4[:, 0:N], ALU.mult)
                c.tt("dve", QKDT[:, 0:N], bQ[:, 0:N], E[:, 0:N], ALU.mult)
                for bl in range(NBk):
                    X, U, On = X_[bl % 2], U_[bl % 2], On_[bl % 2]
                    sqj = E[:, (bl % 2) * 128:(bl % 2) * 128 + 128] if V2 else junk[:, 0:128]
                    sc0 = (4 + 2 * (bl % 2)) if V2 else 0
                    b = c.bank()
                    c.mm(b[:, 0:128], blk(kT, bl), Sbf)
                    c.stt(X, b[:, 0:128], ngam[:, bl * 8 + h:bl * 8 + h + 1], blk(Vtok, bl), ALU.mult, ALU.add)
                    b = c.bank()
                    c.mm(b[:, 0:128], blk(MT, bl), X)
                    c.cp("act", U, b[:, 0:128])
                    bO = c.bank()
                    c.mm(bO[:, 0:128], blk(qgT, bl), Sbf, start=True, stop=False)
                    c.mm(bO[:, 0:128], blk(QKDT, bl), U, start=False, stop=True)
                    c.act(sqj, bO[:, 0:128], AF.Square)
                    c.rsum(ssq[:, sc0:sc0 + 1], sqj)
                    bK = c.bank()
                    c.mm(bK[:, 0:128], blk(Kend, bl), U)
                    c.stt(Sh, Sh, gend[:, bl * 8 + h:bl * 8 + h + 1], bK[:, 0:128], ALU.mult, ALU.add)
                    if bl < NBk - 1:
                        c.cp("act", Sbf, Sh)
                    rsq(ssq[:, sc0 + 1:sc0 + 2], ssq[:, sc0:sc0 + 1], 1.0 / 128)
                    c.ts("dve", On, bO[:, 0:128], ssq[:, sc0 + 1:sc0 + 2], ALU.mult)
                    bT2 = c.bank().bitcast(BF16)
                    c.tr(bT2[:, 0:128], On, idb)
                    c.stt(obr[:, h, bl * 128:(bl + 1) * 128], bT2[:, 0:128], gn[:, 0:1], szb[:, bl * 128:(bl + 1) * 128], ALU.mult, ALU.mult)

        def swa_prompt(s, ti, N):
            NBk = N // 128
            t0 = ti * 512
            qsT = bigv(0, 2048, BF16).rearrange("p (c n) -> p c n", n=512)
            raw = bigv(2048, 512); t1 = bigv(2560, 512); krot = bigv(3072, 2048).rearrange("p (c n) -> p c n", n=512)
            vTf = bigv(5120, 1024).rearrange("p (c n) -> p c n", n=512)
            pT = bigv(6144, 512, BF16).rearrange("p (c n) -> p c n", n=512)
            rec = rstd; ropec = bigv(6656, 1024).rearrange("p (c n) -> p c n", n=512); stg = bigv(7680, 256)
            last = (ti == NTL - 1)
            em.dma("sp", "ropec", ropec[:, 0, 0:N], C_COSP[:, t0:t0 + N], w=[ropec[:, 0, 0:N]])
            em.dma("sp", "ropec", ropec[:, 1, 0:N], C_SINP[:, t0:t0 + N], w=[ropec[:, 1, 0:N]])
            em.restamp("ropec", [ropec[:, :, 0:N]])

            def ev(i, p):
                if i < 12:
                    c.cp("act", raw[:, 0:N], p)
                    b = c.bank()
                    c.mm(b[:, 0:N], rot[:], raw[:, 0:N])
                    c.tt("dve", t1[:, 0:N], raw[:, 0:N], ropec[:, 0, 0:N], ALU.mult)
                    c.tt("dve", raw[:, 0:N], b[:, 0:N], ropec[:, 1, 0:N], ALU.mult)
                    if i < 8:
                        c.tt("dve", qsT[:, i, 0:N], raw[:, 0:N], t1[:, 0:N], ALU.add)
                    else:
                        g = i - 8
                        c.tt("dve", krot[:, g, 0:N], raw[:, 0:N], t1[:, 0:N], ALU.add)
                        c.cp("act", kLo[0:64, g, 128:128 + N], krot[0:64, g, 0:N])
                        c.cp("act", kHi[64:128, g, 128:128 + N], krot[64:128, g, 0:N])
                else:
                    cc = i - 12
                    c.cp("act", vTf[:, cc, 0:N], p)
                    b = c.bank()
                    for bl in range(NBk):
                        c.tr(b[:, bl * 128:(bl + 1) * 128], vTf[:, cc, bl * 128:(bl + 1) * 128], ident[:])
                    bv = b[:, 0:N].rearrange("p (b e d) -> p b e d", e=2, d=64)
                    for dup in range(2):
                        c.cp("dve" if dup else "act", vdup[:, 1:1 + NBk, 2 * cc:2 * cc + 2, dup * 64:(dup + 1) * 64], bv)
                    if last:
                        c.cp("act", stg[:, cc * 128:(cc + 1) * 128], b[:, (NBk - 1) * 128:NBk * 128])
            subs = nat(SW_Q0, 8) + [[(SW_K0 + 64 * g, 64), (SW_K0 + 64 * g, 64)] for g in range(4)] + nat(SW_V0, 2)
            _sw = int(_os.environ.get("DBG_SWA", "9"))
            c.dense(W["w_in"], 16, subs, lambda kc: xn[:, kc, 0:N], N, ev)
            if _sw < 2:
                return
            if last:
                em.dma("sp", "stg", OVP[s * 128:(s + 1) * 128, :], stg[:, :], r=[stg[:, :]], final=True)
                stk = bigv(7936, 256)
                for g in range(4):
                    b = c.bank()
                    c.tr(b[:, 0:128], krot[:, g, N - 128:N], ident[:])
                    c.cp("act", stk[:, g * 64:(g + 1) * 64], b[:, 0:64])
                em.dma("sp", "stk", OKP[s * 128:(s + 1) * 128, :], stk[:, :], r=[stk[:, :]], final=True)
            pTb = [bigv(2048, 512, BF16).rearrange("p (c n) -> p c n", n=512), bigv(2560, 512, BF16).rearrange("p (c n) -> p c n", n=512)]
            its = [(g, qb) for g in range(4) for qb in range(NBk)]

            def kbs_of(qb):
                return [(qb, False)] if (ti == 0 and qb == 0) else [(qb - 1, True), (qb, False)]

            def stageA(i):
                g, qb = its[i]
                for idx, (kb, prev) in enumerate(kbs_of(qb)):
                    bs = c.bank()
                    kcols = slice(128 + kb * 128, 128 + (kb + 1) * 128)
                    for e_ in range(2):
                        c.mm(bs[:, e_ * 256:(e_ + 1) * 256], (kHi if e_ else kLo)[:, g, kcols],
                             qsT[:, 2 * g:2 * g + 2, qb * 128:(qb + 1) * 128])
                    pt = pTb[i % 2][:, idx, :]
                    c.act(pt, bs[:, :], AF.Exp, scale=0.125)
                    c.tt("dve", pt, pt, (ms4b if prev else mi4b)[:, :], ALU.mult)

            def stageB(i):
                g, qb = its[i]
                rec = (rstd if i % 2 == 0 else junk) if int(_os0.environ.get("DBG_V5", "1")) else rstd
                bo = c.bank()
                bd = c.bank()
                kbs = kbs_of(qb)
                for idx, (kb, prev) in enumerate(kbs):
                    pt = pTb[i % 2][:, idx, :]
                    c.mm(bo[:, :], vdup[:, 1 + kb, g, :], pt, start=(idx == 0), stop=(idx == len(kbs) - 1))
                    c.mm(bd[:, :], ones_b[:], pt, start=(idx == 0), stop=(idx == len(kbs) - 1))
                for e_ in range(2):
                    for cc in range(2):
                        hh = 4 * g + 2 * cc + e_
                        col = slice((e_ * 2 + cc) * 128, (e_ * 2 + cc + 1) * 128)
                        c.ts("dve", rec[:, col], bd[:, col], sinkcol[:, hh:hh + 1], ALU.add)
                c.act(rec[:, :], rec[:, :], AF.Ln)
                c.act(rec[:, :], rec[:, :], AF.Exp, scale=-1.0)
                for e_ in range(2):
                    pr = slice(e_ * 64, (e_ + 1) * 64)
                    c.tt("dve", obr[pr, 2 * g:2 * g + 2, qb * 128:(qb + 1) * 128],
                         bo[pr, e_ * 256:(e_ + 1) * 256].rearrange("p (c n) -> p c n", n=128),
                         rec[pr, e_ * 256:(e_ + 1) * 256].rearrange("p (c n) -> p c n", n=128), ALU.mult)

            stageA(0)
            for i in range(len(its)):
                if i + 1 < len(its):
                    stageA(i + 1)
                stageB(i)
            c.cp("act", kLo[0:64, :, 0:128], kLo[0:64, :, N:N + 128])
            c.cp("act", kHi[64:128, :, 0:128], kHi[64:128, :, N:N + 128])
            c.cp("dve", vdup[:, 0, :, :], vdup[:, NBk, :, :])

        mqT = bigv(1024, 1024, BF16).rearrange("p (c n) -> p c n", n=512)
        moT = bigv(2048, 1024, BF16).rearrange("p (c n) -> p c n", n=512)

        def memattn_prompt(N):
            norm(NORM_MEMQ, N)

            def ev(i, p):
                c.cp("act", mqT[:, i, 0:N], p)
            c.dense(W["w_mem_q"], 16, nat(0, 4), lambda kc: xn[:, kc, 0:N], N, ev)
            pT = bigv(0, 512, BF16).rearrange("p (c n) -> p c n", n=512)
            rec = bigv(512, 512)
            recs = [rec, bigv(3072, 512)] if int(_os0.environ.get("DBG_V6", "0")) else [rec, rec]
            pTs2 = [pT, bigv(3584, 512, BF16).rearrange("p (c n) -> p c n", n=512)] if int(_os0.environ.get("DBG_V6", "0")) else [pT, pT]
            for h in range(4):
                rec = recs[h % 2]
                pT = pTs2[h % 2]
                bo = c.bank()
                bd = c.bank()
                for mc in range(2):
                    bs = c.bank()
                    c.mm(bs[:, 0:N], mkT[:, h, mc * 128:(mc + 1) * 128], mqT[:, h, 0:N])
                    c.act(pT[:, mc, 0:N], bs[:, 0:N], AF.Exp, scale=128.0 ** -0.5)
                    c.mm(bo[:, 0:N], mvtok[:, mc, h * 128:(h + 1) * 128], pT[:, mc, 0:N], start=(mc == 0), stop=(mc == 1))
                    c.mm(bd[:, 0:N], ones_b[:], pT[:, mc, 0:N], start=(mc == 0), stop=(mc == 1))
                c.act(rec[:, 0:N], bd[:, 0:N], AF.Ln)
                c.act(rec[:, 0:N], rec[:, 0:N], AF.Exp, scale=-1.0)
                c.tt("dve", moT[:, h, 0:N], bo[:, 0:N], rec[:, 0:N], ALU.mult)
            c.dense(W["w_mem_o"], 4, nat(0, 16), lambda kc: moT[:, kc, 0:N], N, add_res(N))

        def prompt_tile(s, ti):
            N = 512
            row0 = s * T + ti * 512
            import os
            stop = int(os.environ.get("DBG_STOP", "99"))
            if ti == 0:
                issue_x(row0)
            consume_x()
            if stop >= 1:
                ffn("ffn1_w1", "ffn1_w3", "ffn1_w2", NORM_FFN1, N)
            if stop >= 2:
                norm(NORM_MIX, N)
                gdn_prompt(N)
            if stop >= 3:
                branch("w_br_gdn", GG0, True, N)
            if stop >= 4:
                swa_prompt(s, ti, N)
            if stop >= 5:
                branch("w_br_swa", GS0, False, N)
                c.dense(W["w_out"], 16, nat(0, 16), lambda kc: merged[:, kc, 0:N], N, add_res(N))
            if stop >= 6:
                memattn_prompt(N)
            if stop >= 7:
                ffn("ffn2_w1", "ffn2_w3", "ffn2_w2", NORM_FFN2, N)
            if ti + 1 < NTL:
                issue_x(row0 + 512)
            final_out(N, YP[row0:row0 + N, :], True)

        def prompt_seq_end(s):
            em.dma("sp", "Sst", OGP[s * 1024:(s + 1) * 1024, :].rearrange("(h p) v -> p h v", p=128), Sst[:, :, :],
                   r=[Sst[:, :, :]], final=True)
            cst = bigv(0, 3072)
            for q4 in range(6):
                b = c.bank()
                for j in range(4):
                    ch = q4 * 4 + j
                    c.tr(b[0:3, j * 128:(j + 1) * 128], halo[:, ch, :], ident[:])
                c.cp("act", cst[0:3, q4 * 512:(q4 + 1) * 512], b[0:3, :])
            em.dma("sp", "cst", OCP[s * 3:(s + 1) * 3, :], cst[0:3, :], r=[cst[0:3, :]], final=True)

        import os as _os
        _skip = _os.environ.get("DBG_SKIP", "")
        for s in range(NS if DO_PROMPT else 0):
            if "m" not in _skip:
                mem_kv(s)
            c.memset("dve", halo[:, :, :], 0.0)
            c.memset("dve", Sst[:, :, :], 0.0)
            for ti in range(NTL):
                if "t" not in _skip:
                    prompt_tile(s, ti)
            if "e" not in _skip:
                prompt_seq_end(s)


        AX = mybir.AxisListType

        def gdn_sample(N):
            chf = mflat[:, 0:3072]
            Sb = [chf[:, 0:1024].rearrange("p (h v) -> p h v", v=128), chf[:, 1024:2048].rearrange("p (h v) -> p h v", v=128)]
            Sout = chf[:, 2048:3072].rearrange("p (h v) -> p h v", v=128)
            scin = bigv(0, 3072)
            cstT = bigv(3072, 1152).rearrange("p (c n) -> p c n", n=48)
            qT8 = bigv(4224, 128).rearrange("p (h b) -> p h b", b=16); kT8 = bigv(4352, 128).rearrange("p (h b) -> p h b", b=16)
            vT8 = bigv(4480, 128).rearrange("p (h b) -> p h b", b=16); sz8 = bigv(4608, 128).rearrange("p (h b) -> p h b", b=16)
            kb8 = bigv(4736, 128).rearrange("p (h b) -> p h b", b=16)
            pnew = bigv(4864, 384).rearrange("p (c b) -> p c b", b=16)
            bcg = bigv(5248, 128); bcb = bigv(5376, 128); ngb = bigv(5504, 128)
            vtok = bigv(5632, 1024); vbc = bigv(6656, 1024)
            kbc_ = [bigv(7680, 128), bigv(2048, 128)]; gS_ = [bigv(7808, 128), bigv(2176, 128)]; up_ = [bigv(7936, 128), bigv(2304, 128)]
            yq = bigv(8064, 16); e16 = bigv(0, 2048)
            Sout_ = [Sout, mflat[:, 3072:4096].rearrange("p (h v) -> p h v", v=128)]
            bT, gT, gam = g8[:, 0, 0:N], g8[:, 1, 0:N], g8[:, 2, 0:N]
            em.dma("sp", "scin", scin[0:48, :], SC[:, :], w=[scin[0:48, :]])
            em.dma("sp", "ocs_a", OCS.rearrange("(b j) c -> b j c", j=3)[:, 0:2, :], SC.rearrange("(b j) c -> b j c", j=3)[:, 1:3, :], final=True)
            for c0 in range(0, 24, 8):
                b = c.bank()
                for j in range(8):
                    c.tr(b[:, j * 48:(j + 1) * 48], scin[0:48, (c0 + j) * 128:(c0 + j + 1) * 128], ident[0:48, 0:48])
                c.cp("act", cstT[:, c0:c0 + 8, :], b[:, 0:384].rearrange("p (c n) -> p c n", n=48))

            def ev_ba(i, p):
                if i == 0:
                    c.act(bT, p, AF.Sigmoid)
                else:
                    c.act(gT, p, AF.Exp, bias=dtb[:])
                    c.act(gT, gT, AF.Ln, bias=epsb[0:8, 2:3])
                    c.ts("dve", gT, gT, nega[:], ALU.mult)
                    c.act(gam, gT, AF.Exp)
            c.dense(W["w_in"], 16, [[(4096, 8)], [(4104, 8)]], lambda kc: xn[:, kc, 0:N], N, ev_ba)
            for hh in range(8):
                c.ts("dve", dg[:, hh * 16:(hh + 1) * 16], gam, ident[0:8, hh:hh + 1], ALU.mult)
                c.ts("dve", dg[:, 128 + hh * 16:128 + (hh + 1) * 16], bT, ident[0:8, hh:hh + 1], ALU.mult)
            b = c.bank()
            c.mm(b[:, 0:256], ones_f[0:8, :], dg[:, 0:256])
            c.cp("dve", bcg, b[:, 0:128]); c.cp("dve", bcb, b[:, 128:256])
            c.ts("dve", ngb, bcg, -1.0, ALU.mult)

            def conv_s(ch, p):
                c.cp("act", pnew[:, ch, :], p)
                cv = cstT[:, ch, :].rearrange("p (b j) -> p b j", j=3)
                c.ts("dve", yq, p, convw[:, ch * 4 + 3:ch * 4 + 4], ALU.mult)
                for j in (2, 1, 0):
                    c.stt(yq, cv[:, :, j], convw[:, ch * 4 + j:ch * 4 + j + 1], yq, ALU.mult, ALU.add)
                c.act(yq, yq, AF.Silu)

            for h in range(8):
                def ev(i, p, h=h):
                    if i == 0:
                        conv_s(h, p); l2n(qT8[:, h, :], yq, N, epsb[:, 1:2])
                    elif i == 1:
                        conv_s(8 + h, p); l2n(kT8[:, h, :], yq, N, None)
                        c.tt("dve", kb8[:, h, :], kT8[:, h, :], bcb[:, h * 16:(h + 1) * 16], ALU.mult)
                    elif i == 2:
                        conv_s(16 + h, p); c.cp("act", vT8[:, h, :], yq)
                    else:
                        c.act(sz8[:, h, :], p, AF.Silu)
                subs = [[(h * 128, 128)], [(1024 + h * 128, 128)], [(2048 + h * 128, 128)], [(3072 + h * 128, 128)]]
                c.dense(W["w_in"], 16, subs, lambda kc: xn[:, kc, 0:N], N, ev)
            cstg = scin
            for q4 in range(6):
                b = c.bank()
                for j in range(4):
                    c.tr(b[0:16, j * 128:(j + 1) * 128], pnew[:, q4 * 4 + j, :], ident[:])
                c.cp("act", cstg[0:16, q4 * 512:(q4 + 1) * 512], b[0:16, :])
            em.dma("sp", "scin", OCS.rearrange("(b j) c -> b j c", j=3)[:, 2, :], cstg[0:16, :], r=[cstg[0:16, :]], final=True)
            for hf in range(2):
                b = c.bank()
                for j in range(4):
                    c.tr(b[0:16, j * 128:(j + 1) * 128], vT8[:, hf * 4 + j, :], ident[:])
                c.cp("act", vtok[0:16, hf * 512:(hf + 1) * 512], b[0:16, :])
            em.dma("sp", "e16", e16[0:16, :], C_E16[:, :], w=[e16[0:16, :]])
            bo = ps[:, 7, :]
            c.reserved = {7}
            PF = int(_os.environ.get("DBG_W4", "1"))

            def ld_S(b2):
                S2 = Sb[b2 % 2]
                em.dma("sp", "Sb%d" % (b2 % 2), S2, SG[b2 * 1024:(b2 + 1) * 1024, :].rearrange("(h p) v -> p h v", p=128), w=[S2])
            if PF:
                ld_S(0)
            for bb in range(NB):
                S = Sb[bb % 2]
                if PF:
                    if bb + 1 < NB:
                        ld_S(bb + 1)
                else:
                    ld_S(bb)
                for hf in range(2):
                    b = c.bank()
                    c.mm(b[:, :], e16[0:16, bb * 128:(bb + 1) * 128], vtok[0:16, hf * 512:(hf + 1) * 512])
                    c.cp("act", vbc[:, hf * 512:(hf + 1) * 512], b[:, :])
                Sout = Sout_[bb % 2]
                for h in range(8):
                    kbc, gS, up = kbc_[h % 2], gS_[h % 2], up_[h % 2]
                    col = slice(h * 16 + bb, h * 16 + bb + 1)
                    c.ts("dve", kbc, ones_f[:], kT8[:, h, bb:bb + 1], ALU.mult)
                    bk = c.bank()
                    c.mm(bk[:, 0:128], kbc, S[:, h, :])
                    c.stt(up, bk[:, 0:128], ngb[:, col], vbc[:, h * 128:(h + 1) * 128], ALU.mult, ALU.add)
                    c.act(gS, S[:, h, :], AF.Copy, scale=bcg[:, col])
                    c.stt(Sout[:, h, :], up, kb8[:, h, bb:bb + 1], gS, ALU.mult, ALU.add)
                    c.mm(bo[:, col], Sout[:, h, :], qT8[:, h, bb:bb + 1])
                em.dma("sp", "Sout%d" % (bb % 2), OGS[bb * 1024:(bb + 1) * 1024, :].rearrange("(h p) v -> p h v", p=128), Sout, r=[Sout], final=True)
            c.reserved = set()
            sq = sqt[:, 0, 0:128]
            c.act(sq, bo[:, 0:128], AF.Square)
            b = c.bank()
            c.mm(b[:, 0:128], ones_b[:], sq)
            rsq(junk[:, 0:128], b[:, 0:128], 1.0 / 128)
            c.tt("dve", junk[:, 0:128], bo[:, 0:128], junk[:, 0:128], ALU.mult)
            c.stt(obr[:, :, 0:16], junk[:, 0:128].rearrange("p (h b) -> p h b", b=16), gn[:, 0:1], sz8, ALU.mult, ALU.mult)

        def swa_sample(N):
            qs8 = bigv(0, 128).rearrange("p (c b) -> p c b", b=16); kd8 = bigv(128, 64).rearrange("p (c b) -> p c b", b=16)
            vT2 = bigv(192, 32).rearrange("p (c b) -> p c b", b=16); raw = bigv(224, 16); t1 = bigv(240, 16)
            knew = bigv(256, 256); vnew = bigv(512, 256)
            qz = bigv(768, 1024).rearrange("p (b g h) -> p b g h", g=4, h=16)
            Kf = [bigv(1792, 256), bigv(2048, 256)]; Vf = [bigv(2304, 256), bigv(2560, 256)]
            Kd_ = [bigv(2816, 512).rearrange("p (g d) -> p g d", d=128), bigv(4608, 512).rearrange("p (g d) -> p g d", d=128)]
            Vd_ = [bigv(3328, 512).rearrange("p (g d) -> p g d", d=128), bigv(5120, 512).rearrange("p (g d) -> p g d", d=128)]
            KTd_ = [bigv(3840, 512).rearrange("p (g k) -> p g k", k=128), bigv(5632, 512).rearrange("p (g k) -> p g k", k=128)]
            pr__ = [bigv(4352, 128), bigv(6144, 128)]; pTs_ = [bigv(4480, 16), bigv(6272, 16)]; st4_ = [bigv(4496, 8), bigv(6288, 8)]
            sinkc = bigv(4504, 1)

            def ev(i, p):
                if i < 14 and i < 12:
                    c.cp("act", raw, p)
                    b = c.bank()
                    c.mm(b[:, 0:N], rot[:], raw)
                    c.ts("dve", t1, raw, ropes[:, 0:1], ALU.mult)
                    dst = qs8[:, i, :] if i < 8 else kd8[:, i - 8, :]
                    c.stt(dst, b[:, 0:N], ropes[:, 1:2], t1, ALU.mult, ALU.add)
                else:
                    c.cp("act", vT2[:, i - 12, :], p)
            subs = nat(SW_Q0, 8) + [[(SW_K0 + 64 * g, 64), (SW_K0 + 64 * g, 64)] for g in range(4)] + nat(SW_V0, 2)
            c.dense(W["w_in"], 16, subs, lambda kc: xn[:, kc, 0:N], N, ev)
            b = c.bank()
            for g in range(4):
                c.tr(b[0:16, g * 128:(g + 1) * 128], kd8[:, g, :], ident[:])
            c.cp("act", knew[0:16, :].rearrange("p (g d) -> p g d", d=64), b[0:16, :].rearrange("p (g d) -> p g d", d=128)[:, :, 0:64])
            b = c.bank()
            for cc in range(2):
                c.tr(b[0:16, cc * 128:(cc + 1) * 128], vT2[:, cc, :], ident[:])
            c.cp("act", vnew[0:16, :], b[0:16, 0:256])
            for OUT, CIN, new, tok in ((OKS, CK, knew, "OKS"), (OVS, CV, vnew, "OVS")):
                o3 = OUT.rearrange("(b t) d -> b t d", t=128)
                em.dma("sp", "d_" + tok, o3[:, 0:127, :], CIN.rearrange("(b t) d -> b t d", t=128)[:, 1:128, :], tw=[tok + "a"], final=True)
                em.dma("sp", "n_" + tok, o3[:, 127, :], new[0:16, :], r=[new[0:16, :]], tw=[tok + "b"], final=True)
            em.dma("sp", "sinkc", sinkc[0:16, :], SINKC[:, :], w=[sinkc[0:16, :]])
            c.memset("dve", qz, 0.0)
            for g in range(4):
                for j in range(4):
                    e_ = j % 2
                    c.cp("act", qz[e_ * 64:(e_ + 1) * 64, :, g, 4 * g + j], qs8[e_ * 64:(e_ + 1) * 64, 2 * g + j // 2, :])
            for bb in range(NB):
                kf, vf = Kf[bb % 2], Vf[bb % 2]
                Kd, Vd, KTd, pr_, pTs, st4 = Kd_[bb % 2], Vd_[bb % 2], KTd_[bb % 2], pr__[bb % 2], pTs_[bb % 2], st4_[bb % 2]
                em.dma("sp", "kf%d" % (bb % 2), kf, OKS[bb * 128:(bb + 1) * 128, :], w=[kf], tr=["OKSa", "OKSb"])
                em.dma("sp", "vf%d" % (bb % 2), vf, OVS[bb * 128:(bb + 1) * 128, :], w=[vf], tr=["OVSa", "OVSb"])
                for dup in range(2):
                    c.cp("act", Kd[:, :, dup * 64:(dup + 1) * 64], kf.rearrange("p (g d) -> p g d", d=64))
                    c.cp("dve", Vd[:, :, dup * 64:(dup + 1) * 64], vf.rearrange("p (g d) -> p g d", d=64))
                b = c.bank()
                for g in range(4):
                    c.tr(b[:, g * 128:(g + 1) * 128], Kd[:, g, :], ident[:])
                c.cp("act", KTd, b[:, :].rearrange("p (g k) -> p g k", k=128))
                bs = c.bank()
                for g in range(4):
                    c.mm(bs[0:16, 0:128], qz[:, bb, g, :], KTd[:, g, :], start=(g == 0), stop=(g == 3))
                mx, ngm, sm, es = st4[0:16, 0:1], st4[0:16, 1:2], st4[0:16, 2:3], st4[0:16, 3:4]
                em.op("dve", lambda e, mx=mx, bs=bs: e.reduce_max(out=mx, in_=bs[0:16, 0:128], axis=AX.X), r=[bs[0:16, 0:128]], w=[mx])
                c.ts("dve", mx, mx, 0.125, ALU.mult, sinkc[0:16, :], ALU.max)
                c.ts("dve", ngm, mx, -1.0, ALU.mult)
                c.act(pr_[0:16, :], bs[0:16, 0:128], AF.Exp, scale=0.125, bias=ngm)
                c.rsum(sm, pr_[0:16, :])
                c.act(es, sinkc[0:16, :], AF.Exp, bias=ngm)
                c.tt("dve", sm, sm, es, ALU.add)
                em.op("dve", lambda e, sm=sm: e.reciprocal(out=sm, in_=sm), r=[sm], w=[sm])
                c.ts("dve", pr_[0:16, :], pr_[0:16, :], sm, ALU.mult)
                b = c.bank()
                c.tr(b[:, 0:16], pr_[0:16, :], ident[0:16, 0:16])
                c.cp("act", pTs, b[:, 0:16])
                bo = c.bank()
                for g in range(4):
                    c.mm(bo[:, g * 16:(g + 1) * 16], Vd[:, g, :], pTs)
                for g in range(4):
                    for e_ in range(2):
                        prt = slice(e_ * 64, (e_ + 1) * 64)
                        c.cp("dve" if e_ else "act", obr[prt, 2 * g:2 * g + 2, bb],
                             bo[prt, g * 16 + 4 * g + e_:g * 16 + 4 * g + e_ + 3:2])

        def memattn_sample(N):
            norm(NORM_MEMQ, N)
            mq8 = bigv(4096, 64).rearrange("p (h b) -> p h b", b=16)
            qz4 = bigv(4160, 256).rearrange("p (b h k) -> p b h k", h=4, k=4)
            Kc = [bigv(0, 1024).rearrange("p (c d) -> p c d", d=512), bigv(1024, 1024).rearrange("p (c d) -> p c d", d=512)]
            Vc = [bigv(2048, 1024).rearrange("p (c d) -> p c d", d=512), bigv(3072, 1024).rearrange("p (c d) -> p c d", d=512)]
            KTm_ = [bigv(4416, 1024).rearrange("p (h m) -> p h m", m=256), bigv(5760, 1024).rearrange("p (h m) -> p h m", m=256)]
            pm_ = [bigv(5440, 256), bigv(6784, 256)]
            pmT_ = [bigv(5696, 8).rearrange("p (c h) -> p c h", h=4), bigv(7040, 8).rearrange("p (c h) -> p c h", h=4)]
            st4_ = [bigv(5704, 8), bigv(7048, 8)]

            def ev(i, p):
                c.cp("act", mq8[:, i, :], p)
            c.dense(W["w_mem_q"], 16, nat(0, 4), lambda kc: xn[:, kc, 0:N], N, ev)
            c.memset("dve", qz4, 0.0)
            for h in range(4):
                c.cp("act", qz4[:, :, h, h], mq8[:, h, :])
            bo = ps[:, 7, :]
            c.reserved = {7}
            for bb in range(NB):
                kc_, vc_ = Kc[bb % 2], Vc[bb % 2]
                KTm, pm, pmT, st4 = KTm_[bb % 2], pm_[bb % 2], pmT_[bb % 2], st4_[bb % 2]
                em.dma("sp", "kc%d" % (bb % 2), kc_, CMK[bb * 256:(bb + 1) * 256, :].rearrange("(c p) d -> p c d", p=128), w=[kc_])
                em.dma("sp", "vc%d" % (bb % 2), vc_, CMV[bb * 256:(bb + 1) * 256, :].rearrange("(c p) d -> p c d", p=128), w=[vc_])
                for h in range(4):
                    b = c.bank()
                    for mc in range(2):
                        c.tr(b[:, mc * 128:(mc + 1) * 128], kc_[:, mc, h * 128:(h + 1) * 128], ident[:])
                    c.cp("act" if h % 2 else "dve", KTm[:, h, :], b[:, 0:256])
                bs = c.bank()
                for h in range(4):
                    c.mm(bs[0:4, 0:256], qz4[:, bb, h, :], KTm[:, h, :], start=(h == 0), stop=(h == 3))
                mx, ngm, sm = st4[0:4, 0:1], st4[0:4, 1:2], st4[0:4, 2:3]
                sc_ = 128.0 ** -0.5
                em.op("dve", lambda e, mx=mx, bs=bs: e.reduce_max(out=mx, in_=bs[0:4, 0:256], axis=AX.X), r=[bs[0:4, 0:256]], w=[mx])
                c.ts("dve", ngm, mx, -sc_, ALU.mult)
                c.act(pm[0:4, :], bs[0:4, 0:256], AF.Exp, scale=sc_, bias=ngm)
                c.rsum(sm, pm[0:4, :])
                em.op("dve", lambda e, sm=sm: e.reciprocal(out=sm, in_=sm), r=[sm], w=[sm])
                c.ts("dve", pm[0:4, :], pm[0:4, :], sm, ALU.mult)
                b = c.bank()
                for mc in range(2):
                    c.tr(b[:, mc * 4:(mc + 1) * 4], pm[0:4, mc * 128:(mc + 1) * 128], ident[0:4, 0:4])
                c.cp("act", pmT, b[:, 0:8].rearrange("p (c h) -> p c h", h=4))
                for h in range(4):
                    for mc in range(2):
                        c.mm(bo[:, h * 16 + bb:h * 16 + bb + 1], vc_[:, mc, h * 128:(h + 1) * 128], pmT[:, mc, h:h + 1],
                             start=(mc == 0), stop=(mc == 1))
            c.reserved = set()
            c.cp("act", moT[:, :, 0:16], bo[:, 0:64].rearrange("p (h b) -> p h b", b=16))
            c.dense(W["w_mem_o"], 4, nat(0, 16), lambda kc: moT[:, kc, 0:N], N, add_res(N))

        def sample_pass():
            N = NB
            load_x(XS[:, :], N)
            ffn("ffn1_w1", "ffn1_w3", "ffn1_w2", NORM_FFN1, N)
            norm(NORM_MIX, N)
            gdn_sample(N)
            branch("w_br_gdn", GG0, True, N)
            swa_sample(N)
            branch("w_br_swa", GS0, False, N)
            c.dense(W["w_out"], 16, nat(0, 16), lambda kc: merged[:, kc, 0:N], N, add_res(N))
            memattn_sample(N)
            ffn("ffn2_w1", "ffn2_w3", "ffn2_w2", NORM_FFN2, N)
            final_out(N, YS[:, :], True)

        if DO_SAMPLE:
            sample_pass()
        em.build()
    return nc


_CACHE = {}


def _consts(T):
    idn = np.eye(128, dtype=np.float32)
    p = np.arange(128)
    partner = np.where((p % 64) < 32, p + 32, p - 32)
    rot = np.zeros((128, 128), np.float32)
    rot[partner, p] = 1.0
    i = np.arange(128)[:, None]
    j = np.arange(128)[None, :]
    ms = (i > j).astype(np.float32)
    mi = (j >= i).astype(np.float32)
    e8 = np.zeros((8, 8, 128), np.float32)
    for h in range(8):
        e8[h, h, :] = 1.0
    e16 = np.zeros((16, 16, 128), np.float32)
    for b in range(16):
        e16[b, b, :] = 1.0
    half = 32
    inv_freq = (np.float32(10000.0) ** (-np.arange(half, dtype=np.float32) / np.float32(half))).astype(np.float32)
    fi = (p % 64) % 32
    sign = np.where((p % 64) < 32, -1.0, 1.0).astype(np.float32)

    def tables(pos):
        ang = pos.astype(np.float32)[None, :] * inv_freq[fi][:, None]
        ang = ang.astype(np.float32)
        return np.cos(ang).astype(np.float32), (np.sin(ang).astype(np.float32) * sign[:, None]).astype(np.float32)
    cp_, sp_ = tables(np.arange(T))
    cs_, ss_ = tables(np.array([16384]))
    return {
        "c_ident": idn, "c_rot": rot, "c_mstrict4": np.tile(ms, (1, 4)), "c_minclT4": np.tile(mi, (1, 4)),
        "c_ident4": np.tile(idn, (1, 4)), "c_e8": e8.reshape(8, 1024), "c_e16": e16.reshape(16, 2048),
        "c_cosp": cp_, "c_sinp": sp_, "c_coss": cs_, "c_sins": ss_,
    }


def run(inputs, NS, T, NB, n_cores, DO_SAMPLE=True, DO_PROMPT=True):
    f = lambda a: np.ascontiguousarray(np.asarray(a, dtype=np.float32))
    key = (NS, T, NB, DO_SAMPLE, DO_PROMPT)
    if key not in _CACHE:
        _CACHE[key] = build_program(NS, T, NB, DO_SAMPLE, DO_PROMPT)
    nc = _CACHE[key]
    shared = {}
    for nm in ("ffn1_w1", "ffn1_w3", "ffn1_w2", "w_in", "w_br_gdn", "w_br_swa", "w_out", "w_mem_q", "w_mem_k", "w_mem_v",
               "w_mem_o", "ffn2_w1", "ffn2_w3", "ffn2_w2"):
        shared[nm] = f(inputs[nm][0])
    norms = [inputs[k][0] for k in ("norm_ffn1", "norm_mix", "norm_mem_q", "norm_mem_kv", "norm_ffn2")] + [inputs["norm_final"]]
    g = np.stack([f(n).reshape(16, 128).T for n in norms], 1).reshape(128, 96)
    shared["gains"] = f(g)
    cw = f(inputs["conv_w"][0])
    shared["convw"] = f(cw.reshape(4, 24, 128).transpose(2, 1, 0).reshape(128, 96))
    shared["alog"] = f(inputs["gdn_A_log"][0]).reshape(8, 1)
    shared["dtb"] = f(inputs["gdn_dt_bias"][0]).reshape(8, 1)
    shared["gnorm"] = f(inputs["gdn_norm"][0]).reshape(128, 1)
    shared["sinks"] = f(inputs["swa_sinks"][0]).reshape(1, 16)
    shared["sinks_col"] = f(inputs["swa_sinks"][0]).reshape(16, 1)
    shared.update(_consts(T))
    in_maps = []
    for ci in range(n_cores):
        m = dict(shared)
        m["xp"] = f(inputs["x_prompt"][ci * NS:(ci + 1) * NS]).reshape(NS * T, D)
        m["xs"] = f(inputs["x_sample"][ci * NB:(ci + 1) * NB]).reshape(NB, D)
        m["sgdn"] = f(inputs["state_gdn"][0, ci * NB:(ci + 1) * NB]).reshape(NB * 1024, 128)
        m["sconv"] = f(inputs["state_conv"][0, ci * NB:(ci + 1) * NB]).reshape(NB * 3, 3072)
        m["ck"] = f(inputs["cache_swa_k"][0, ci * NB:(ci + 1) * NB]).reshape(NB * 128, 256)
        m["cv"] = f(inputs["cache_swa_v"][0, ci * NB:(ci + 1) * NB]).reshape(NB * 128, 256)
        m["cmk"] = f(inputs["cache_mem_k"][0, ci * NB:(ci + 1) * NB]).reshape(NB * 256, 512)
        m["cmv"] = f(inputs["cache_mem_v"][0, ci * NB:(ci + 1) * NB]).reshape(NB * 256, 512)
        m["memp"] = f(inputs["mem_prompt"][ci * NS:(ci + 1) * NS]).reshape(NS * 256, D)
        in_maps.append(m)
    res = run_bass_kernel_spmd(nc, in_maps, core_ids=list(range(n_cores)))
    R = res.results
    cat = lambda k: np.concatenate([np.asarray(r[k]) for r in R], 0)
    B = NS * n_cores
    BS = NB * n_cores
    y_p = cat("y_p").reshape(B, T, D)
    y_s = cat("y_s").reshape(BS, 1, D)
    outs = (y_p, y_s,
            cat("o_gdn_p").reshape(1, B, 8, 128, 128), cat("o_conv_p").reshape(1, B, 3, 3072),
            cat("o_k_p").reshape(1, B, 128, 4, 64), cat("o_v_p").reshape(1, B, 128, 4, 64),
            cat("o_mk_p").reshape(1, B, 256, 4, 128), cat("o_mv_p").reshape(1, B, 256, 4, 128),
            cat("o_gdn_s").reshape(1, BS, 8, 128, 128), cat("o_conv_s").reshape(1, BS, 3, 3072),
            cat("o_k_s").reshape(1, BS, 128, 4, 64), cat("o_v_s").reshape(1, BS, 128, 4, 64))
    return tuple(np.ascontiguousarray(o, dtype=np.float32) for o in outs)


def kernel(**inputs):
    return run(inputs, 2, 2048, 16, 8)
```
